# Optimizing a Trainium2 kernel written in Bass

```python
import math
import jax, jax.numpy as jnp
from jax import lax
import numpy as np

D_MODEL = 2048
BATCH = 8
SEQ = 2048
DEPTH = 2

HEAD_DIM = 128
N_HEAD_SLOTS = D_MODEL // HEAD_DIM
H_GLA = (5 * N_HEAD_SLOTS) // 16
H_FOX = (5 * N_HEAD_SLOTS) // 16
H_GDN = N_HEAD_SLOTS - H_GLA - H_FOX
GDN_DK = 128
GDN_DV = 128
GLA_DK = 64
GLA_DV = 128
FOX_D = 128
GDN_CONV = 4
GLA_RANK = 16
GLA_NORMALIZER = 16.0
CHUNK = 64
Q_BLOCK = 128
D_FF = 4 * D_MODEL
FFN_CONV = 3
EPS = 1e-6

GDN_QK = H_GDN * GDN_DK
GDN_V = H_GDN * GDN_DV
GLA_QK = H_GLA * GLA_DK
GLA_V = H_GLA * GLA_DV
FOX_W = H_FOX * FOX_D
MIX_WIDTH = GDN_V + GLA_V + FOX_W
IN_WIDTHS = (GDN_QK, GDN_QK, GDN_V, GDN_V, H_GDN, H_GDN,
             GLA_QK, GLA_QK, GLA_V, GLA_V, GLA_RANK,
             FOX_W, FOX_W, FOX_W, H_FOX)
N_IN = sum(IN_WIDTHS)

kernel_name = 'hymba_style_gdn_gla_fox_convffn'


def _split_points():
    pts, acc = [], 0
    for w in IN_WIDTHS[:-1]:
        acc += w
        pts.append(acc)
    return pts


def rms_norm(x, w):
    xf = x.astype(jnp.float32)
    y = xf * lax.rsqrt(jnp.mean(xf * xf, axis=-1, keepdims=True) + EPS)
    return (y * w.astype(jnp.float32)).astype(x.dtype)


def l2norm(x):
    xf = x.astype(jnp.float32)
    return (xf * lax.rsqrt(jnp.sum(xf * xf, axis=-1, keepdims=True) + EPS)).astype(x.dtype)


def causal_depthwise_conv(x, w):
    width, ch = w.shape
    return lax.conv_general_dilated(
        x, w[:, None, :].astype(x.dtype), window_strides=(1,), padding=[(width - 1, 0)],
        dimension_numbers=('NWC', 'WIO', 'NWC'), feature_group_count=ch)


def to_heads(t, n_heads, d):
    b, s, _ = t.shape
    return t.reshape(b, s, n_heads, d).transpose(0, 2, 1, 3)


def from_heads(t):
    b, h, s, d = t.shape
    return t.transpose(0, 2, 1, 3).reshape(b, s, h * d)


def gated_delta_chunked(q, k, v, g, beta):
    dtype = v.dtype
    q, k, v, g, beta = [t.astype(jnp.float32) for t in (q, k, v, g, beta)]
    b, h, s, dk = q.shape
    dv = v.shape[-1]
    n = s // CHUNK
    rs = lambda t: t.reshape(b, h, n, CHUNK, *t.shape[3:])
    q, k, v, g, beta = rs(q), rs(k), rs(v), rs(g), rs(beta)
    gc = jnp.cumsum(g, axis=-1)
    tri_incl = jnp.tril(jnp.ones((CHUNK, CHUNK), bool))
    tri_strict = jnp.tril(jnp.ones((CHUNK, CHUNK), bool), -1)
    decay = jnp.exp(jnp.where(tri_incl, gc[..., :, None] - gc[..., None, :], -jnp.inf))
    kb = k * beta[..., None]
    a_mat = jnp.where(tri_strict, jnp.einsum('bhnid,bhnjd->bhnij', kb, k) * decay, 0.0)
    rhs = jnp.concatenate([v * beta[..., None], kb * jnp.exp(gc)[..., None]], axis=-1)
    sol = lax.linalg.triangular_solve(a_mat, rhs, left_side=True, lower=True, unit_diagonal=True)
    u, w = sol[..., :dv], sol[..., dv:]
    qk = jnp.where(tri_incl, jnp.einsum('bhnid,bhnjd->bhnij', q, k) * decay, 0.0)

    def step(state, xs):
        q_c, k_c, u_c, w_c, qk_c, g_c = xs
        v_new = u_c - jnp.einsum('bhcd,bhde->bhce', w_c, state)
        o_c = (jnp.einsum('bhcd,bhde->bhce', q_c * jnp.exp(g_c)[..., None], state)
               + jnp.einsum('bhij,bhje->bhie', qk_c, v_new))
        g_last = g_c[..., -1]
        state = (state * jnp.exp(g_last)[..., None, None]
                 + jnp.einsum('bhcd,bhce->bhde', k_c * jnp.exp(g_last[..., None] - g_c)[..., None], v_new))
        return state, o_c

    xs = tuple(jnp.moveaxis(t, 2, 0) for t in (q, k, u, w, qk, gc))
    s0 = jnp.zeros((b, h, dk, dv), jnp.float32)
    _, o = lax.scan(step, s0, xs)
    return jnp.moveaxis(o, 0, 2).reshape(b, h, s, dv).astype(dtype)


def gla_chunked(q, k, v, gk):
    dtype = v.dtype
    q, k, v, gk = [t.astype(jnp.float32) for t in (q, k, v, gk)]
    b, h, s, dk = q.shape
    dv = v.shape[-1]
    n = s // CHUNK
    rs = lambda t: t.reshape(b, h, n, CHUNK, t.shape[-1])
    q, k, v, gk = rs(q), rs(k), rs(v), rs(gk)
    bc = jnp.cumsum(gk, axis=-2)
    q_dec = q * jnp.exp(bc)
    k_inv = k * jnp.exp(-bc)
    k_to_end = k * jnp.exp(bc[..., -1:, :] - bc)
    tri_incl = jnp.tril(jnp.ones((CHUNK, CHUNK), bool))
    attn = jnp.where(tri_incl, jnp.einsum('bhnid,bhnjd->bhnij', q_dec, k_inv), 0.0)
    o_intra = jnp.einsum('bhnij,bhnje->bhnie', attn, v)
    decay_last = jnp.exp(bc[..., -1, :])

    def step(state, xs):
        q_c, k_c, v_c, d_c = xs
        o_c = jnp.einsum('bhcd,bhde->bhce', q_c, state)
        state = state * d_c[..., None] + jnp.einsum('bhcd,bhce->bhde', k_c, v_c)
        return state, o_c

    xs = tuple(jnp.moveaxis(t, 2, 0) for t in (q_dec, k_to_end, v, decay_last))
    s0 = jnp.zeros((b, h, dk, dv), jnp.float32)
    _, o_inter = lax.scan(step, s0, xs)
    o = o_intra + jnp.moveaxis(o_inter, 0, 2)
    return o.reshape(b, h, s, dv).astype(dtype)


def gdn_mixer(q, k, v, z, beta_logit, alpha_in, conv_w, a_log, dt_bias, norm_w):
    b, s, _ = q.shape
    qkv = jax.nn.silu(causal_depthwise_conv(jnp.concatenate([q, k, v], axis=-1), conv_w))
    q, k, v = jnp.split(qkv, [GDN_QK, 2 * GDN_QK], axis=-1)
    q = l2norm(to_heads(q, H_GDN, GDN_DK)) * (GDN_DK ** -0.5)
    k = l2norm(to_heads(k, H_GDN, GDN_DK))
    v = to_heads(v, H_GDN, GDN_DV)
    beta = jax.nn.sigmoid(beta_logit.astype(jnp.float32)).transpose(0, 2, 1)
    g = -(jnp.exp(a_log.astype(jnp.float32))
          * jax.nn.softplus(alpha_in.astype(jnp.float32) + dt_bias.astype(jnp.float32))).transpose(0, 2, 1)
    o = gated_delta_chunked(q, k, v, g, beta).transpose(0, 2, 1, 3)
    o = rms_norm(o, norm_w) * jax.nn.silu(z.reshape(b, s, H_GDN, GDN_DV))
    return o.reshape(b, s, GDN_V)


def gla_mixer(q, k, v, g_out, gate_lr, w_gate, b_gate, norm_w):
    b, s, _ = q.shape
    gk = jax.nn.log_sigmoid((gate_lr @ w_gate + b_gate).astype(jnp.float32)) / GLA_NORMALIZER
    q = to_heads(q, H_GLA, GLA_DK) * (GLA_DK ** -0.5)
    k = to_heads(k, H_GLA, GLA_DK)
    v = to_heads(v, H_GLA, GLA_DV)
    gk = to_heads(gk, H_GLA, GLA_DK)
    o = gla_chunked(q, k, v, gk).transpose(0, 2, 1, 3)
    o = rms_norm(o, norm_w) * jax.nn.silu(g_out.reshape(b, s, H_GLA, GLA_DV))
    return o.reshape(b, s, GLA_V)


def fox_mixer(q, k, v, f_logit, f_bias):
    s_len = q.shape[1]
    q = to_heads(q, H_FOX, FOX_D)
    k = to_heads(k, H_FOX, FOX_D)
    v = to_heads(v, H_FOX, FOX_D)
    log_f = jax.nn.log_sigmoid(f_logit.astype(jnp.float32) + f_bias.astype(jnp.float32)).transpose(0, 2, 1)
    c = jnp.cumsum(log_f, axis=-1)
    scale = FOX_D ** -0.5
    outs = []
    for blk in range(s_len // Q_BLOCK):
        lo, hi = blk * Q_BLOCK, (blk + 1) * Q_BLOCK
        scores = (jnp.einsum('bhqd,bhkd->bhqk', q[:, :, lo:hi], k[:, :, :hi]).astype(jnp.float32) * scale
                  + c[:, :, lo:hi, None] - c[:, :, None, :hi])
        causal = (lo + jnp.arange(Q_BLOCK))[:, None] >= jnp.arange(hi)[None, :]
        p = jax.nn.softmax(jnp.where(causal, scores, -jnp.inf), axis=-1).astype(v.dtype)
        outs.append(jnp.einsum('bhqk,bhkd->bhqd', p, v[:, :, :hi]))
    return from_heads(jnp.concatenate(outs, axis=2))


def hybrid_layer(x, w_in, conv_gdn, gdn_a_log, gdn_dt_bias, gdn_norm, gla_w_gate, gla_b_gate,
                 gla_norm, fox_f_bias, w_out, norm_pre_mix, norm_post_mix, norm_pre_ffn,
                 norm_post_ffn, w_up, conv_ffn, conv_ffn_bias, w_down):
    h = rms_norm(x, norm_pre_mix)
    proj = h @ w_in
    (gdn_q, gdn_k, gdn_v, gdn_z, gdn_b, gdn_a,
     gla_q, gla_k, gla_v, gla_g, gla_lr,
     fox_q, fox_k, fox_v, fox_f) = jnp.split(proj, _split_points(), axis=-1)
    o = jnp.concatenate([
        gdn_mixer(gdn_q, gdn_k, gdn_v, gdn_z, gdn_b, gdn_a, conv_gdn, gdn_a_log, gdn_dt_bias, gdn_norm),
        gla_mixer(gla_q, gla_k, gla_v, gla_g, gla_lr, gla_w_gate, gla_b_gate, gla_norm),
        fox_mixer(fox_q, fox_k, fox_v, fox_f, fox_f_bias),
    ], axis=-1)
    x = x + rms_norm(o @ w_out, norm_post_mix)
    h = rms_norm(x, norm_pre_ffn)
    u = causal_depthwise_conv(h @ w_up, conv_ffn) + conv_ffn_bias
    gate, val = jnp.split(u, 2, axis=-1)
    y = (jax.nn.gelu(gate, approximate=True) * val) @ w_down
    return x + rms_norm(y, norm_post_ffn)


def setup_inputs(seed: int = 0) -> dict:
    key = jax.random.key(seed)
    ks = jax.random.split(key, 20)
    f32 = jnp.float32
    nrm = lambda k, shape, s: jax.random.normal(k, shape, f32) * s
    gain = lambda k, n: 1.0 + 0.02 * jax.random.normal(k, (DEPTH, n), f32)
    dt = jnp.exp(jax.random.uniform(ks[4], (DEPTH, H_GDN), f32, math.log(1e-3), math.log(1e-1)))
    return {
        'x': jax.random.normal(ks[0], (BATCH, SEQ, D_MODEL), f32),
        'w_in': nrm(ks[1], (DEPTH, D_MODEL, N_IN), D_MODEL ** -0.5),
        'conv_gdn': nrm(ks[2], (DEPTH, GDN_CONV, 2 * GDN_QK + GDN_V), GDN_CONV ** -0.5),
        'gdn_a_log': jnp.log(jax.random.uniform(ks[3], (DEPTH, H_GDN), f32, 1.0, 16.0)),
        'gdn_dt_bias': dt + jnp.log(-jnp.expm1(-dt)),
        'gdn_norm': gain(ks[5], GDN_DV),
        'gla_w_gate': nrm(ks[6], (DEPTH, GLA_RANK, GLA_QK), GLA_RANK ** -0.5),
        'gla_b_gate': nrm(ks[7], (DEPTH, GLA_QK), 0.02),
        'gla_norm': gain(ks[8], GLA_DV),
        'fox_f_bias': jax.random.uniform(ks[9], (DEPTH, H_FOX), f32, 1.0, 5.0),
        'w_out': nrm(ks[10], (DEPTH, MIX_WIDTH, D_MODEL), MIX_WIDTH ** -0.5),
        'norm_pre_mix': gain(ks[11], D_MODEL),
        'norm_post_mix': gain(ks[12], D_MODEL),
        'norm_pre_ffn': gain(ks[13], D_MODEL),
        'norm_post_ffn': gain(ks[14], D_MODEL),
        'w_up': nrm(ks[15], (DEPTH, D_MODEL, 2 * D_FF), D_MODEL ** -0.5),
        'conv_ffn': nrm(ks[16], (DEPTH, FFN_CONV, 2 * D_FF), FFN_CONV ** -0.5),
        'conv_ffn_bias': nrm(ks[17], (DEPTH, 2 * D_FF), 0.02),
        'w_down': nrm(ks[18], (DEPTH, D_FF, D_MODEL), D_FF ** -0.5),
    }


def reference(x, w_in, conv_gdn, gdn_a_log, gdn_dt_bias, gdn_norm, gla_w_gate, gla_b_gate,
              gla_norm, fox_f_bias, w_out, norm_pre_mix, norm_post_mix, norm_pre_ffn,
              norm_post_ffn, w_up, conv_ffn, conv_ffn_bias, w_down):
    for i in range(DEPTH):
        x = hybrid_layer(x, w_in[i], conv_gdn[i], gdn_a_log[i], gdn_dt_bias[i], gdn_norm[i],
                         gla_w_gate[i], gla_b_gate[i], gla_norm[i], fox_f_bias[i], w_out[i],
                         norm_pre_mix[i], norm_post_mix[i], norm_pre_ffn[i], norm_post_ffn[i],
                         w_up[i], conv_ffn[i], conv_ffn_bias[i], w_down[i])
    return x
```

```python
import numpy as np
import concourse.bass as bass
import concourse.mybir as mybir
from concourse.bass_utils import run_bass_kernel_spmd

F32 = mybir.dt.float32
BF16 = mybir.dt.bfloat16
ALU = mybir.AluOpType
AF = mybir.ActivationFunctionType
AX = mybir.AxisListType

SBUF_BASE = 16512 + 2048
SBUF_END = 229376


class Dep:
    __slots__ = ("lw", "rd", "name", "excl")

    def __init__(self, name=""):
        self.lw = None
        self.rd = []
        self.name = name
        self.excl = False


class Buf:
    def __init__(self, name, handle, nparts=1, off=None, size=None):
        self.name = name
        self.h = handle
        self.parts = [Dep(f"{name}.{i}") for i in range(nparts)]
        self.off = off
        self.size = size

    def __getitem__(self, key):
        return self.h[key]

    def part(self, i):
        return self.parts[i]

    @property
    def all(self):
        return self.parts


class Op:
    __slots__ = ("eng", "fn", "waits", "is_dma", "slot", "seq", "signals", "tick", "idx")


class Prog:
    ENGS = ("pe", "act", "dve", "pool", "sp")
    NSLOT = {"sp": 8, "pool": 8, "act": 6}

    def __init__(self, nc):
        self.nc = nc
        self.ops = []
        self.dma_count = {q: 0 for q in self.NSLOT}
        self.free = [(SBUF_BASE, SBUF_END - SBUF_BASE)]
        self.retired = []
        self.nbuf = 0
        self.psum_banks = None

    def sb(self, name, shape, dtype, nparts=1):
        esz = 4 if dtype == F32 else 2
        n = 1
        for s in shape[1:]:
            n *= s
        size = (n * esz + 63) // 64 * 64
        for i, (o, s) in enumerate(self.free):
            if s >= size:
                off = o
                if s == size:
                    self.free.pop(i)
                else:
                    self.free[i] = (o + size, s - size)
                break
        else:
            raise RuntimeError(f"SBUF OOM allocating {name} {shape} size {size}; free={self.free}")
        self.nbuf += 1
        h = self.nc.alloc_sbuf_tensor_at(f"{name}_{self.nbuf}", list(shape), dtype, offset=off)
        b = Buf(name, h, nparts, off, size)
        keep = []
        for (ro, rs, deps) in self.retired:
            if ro < off + size and off < ro + rs:
                for d in deps:
                    for p in b.parts:
                        if d.lw is not None:
                            p.rd.append(d.lw)
                        p.rd.extend(d.rd)
                if not (off <= ro and ro + rs <= off + size):
                    keep.append((ro, rs, deps))
            else:
                keep.append((ro, rs, deps))
        self.retired = keep
        return b

    def release(self, *bufs):
        for b in bufs:
            self.retired.append((b.off, b.size, list(b.parts)))
            self.free.append((b.off, b.size))
        self.free.sort()
        m = []
        for o, s in self.free:
            if m and m[-1][0] + m[-1][1] == o:
                m[-1] = (m[-1][0], m[-1][1] + s)
            else:
                m.append((o, s))
        self.free = m

    def dram(self, name, shape, dtype, kind="Internal", nparts=1):
        h = self.nc.dram_tensor(name, list(shape), dtype, kind=kind)
        return Buf(name, h, nparts)

    def psum(self, name, shape, dtype=F32, nparts=1):
        h = self.nc.alloc_psum_tensor(name, list(shape), dtype)
        b = Buf(name, h, nparts)
        for p in b.parts:
            p.excl = True
        return b

    def _deps(self, reads, writes):
        raw, war = set(), set()
        for d in reads:
            if d.lw is not None:
                raw.add(d.lw)
        for d in writes:
            if d.lw is not None:
                war.add(d.lw)
            for r in d.rd:
                war.add(r)
        return raw, war

    def _flat(self, lst):
        out = []
        for x in lst:
            if isinstance(x, Buf):
                out.extend(x.parts)
            elif isinstance(x, Dep):
                out.append(x)
            else:
                out.extend(self._flat(x))
        return out

    def op(self, eng, fn, reads=(), writes=()):
        reads = self._flat(reads)
        writes = self._flat(writes)
        ex = [d for d in reads if d.excl]
        if ex:
            reads = [d for d in reads if not d.excl]
            writes = writes + [d for d in ex if d not in writes]
        o = Op()
        o.eng = eng
        o.fn = fn
        o.is_dma = False
        o.signals = False
        o.tick = None
        o.idx = len(self.ops)
        raw, war = self._deps(reads, writes)
        waits = []
        for p in raw:
            if p.is_dma or p.eng != eng or eng != "pe":
                waits.append(p)
        for p in war:
            if p in raw:
                continue
            if p.is_dma or p.eng != eng:
                waits.append(p)
        o.waits = waits
        for p in waits:
            if not p.is_dma:
                p.signals = True
        for d in reads:
            d.rd.append(o)
        for d in writes:
            d.lw = o
            d.rd = []
        self.ops.append(o)
        return o

    def dma(self, q, out, in_, reads=(), writes=()):
        reads = self._flat(reads)
        writes = self._flat(writes)
        o = Op()
        o.eng = q
        o.fn = (out, in_)
        o.is_dma = True
        o.signals = False
        o.tick = None
        o.idx = len(self.ops)
        n = self.dma_count[q]
        self.dma_count[q] += 1
        o.slot = n % self.NSLOT[q]
        o.seq = n // self.NSLOT[q]
        raw, war = self._deps(reads, writes)
        waits = list(raw | war)
        o.waits = waits
        for p in waits:
            if not p.is_dma:
                p.signals = True
        for d in reads:
            d.rd.append(o)
        for d in writes:
            d.lw = o
            d.rd = []
        self.ops.append(o)
        return o

    CLIM = 4000
    DLIM = 250

    def emit(self):
        nc = self.nc
        import contextlib
        es = contextlib.ExitStack()
        with es:
            semtab = {}

            def getsem(key):
                if key not in semtab:
                    semtab[key] = es.enter_context(nc.semaphore("s_" + "_".join(str(k) for k in key)))
                return semtab[key]

            cnt = {e: 0 for e in ("pe", "act", "dve", "pool")}
            for o in self.ops:
                if not o.is_dma and o.signals:
                    c = cnt[o.eng]
                    cnt[o.eng] += 1
                    o.tick = (c // self.CLIM, c % self.CLIM + 1)
            streams = {e: [] for e in self.ENGS}
            seen = {e: {} for e in self.ENGS}

            def sig(p):
                if p.is_dma:
                    ep, sq = p.seq // self.DLIM, p.seq % self.DLIM
                    return ("d", p.eng, p.slot), (ep, 16 * (sq + 1)), getsem(("d", p.eng, p.slot, ep))
                return ("c", p.eng), p.tick, getsem(("c", p.eng, p.tick[0]))

            last_dma = {}
            for o in self.ops:
                st = streams[o.eng]
                sn = seen[o.eng]
                best = {}
                plist = list(o.waits)
                if o.is_dma:
                    prev = last_dma.get((o.eng, o.slot))
                    if prev is not None:
                        plist.append(prev)
                    last_dma[(o.eng, o.slot)] = o
                for p in plist:
                    k, v, s = sig(p)
                    if sn.get(k, (-1, 0)) >= v:
                        continue
                    if k not in best or best[k][0] < v:
                        best[k] = (v, s)
                for k, (v, s) in best.items():
                    sn[k] = v
                    st.append(("w", s, v[1]))
                if o.is_dma:
                    ep = o.seq // self.DLIM
                    st.append(("d", o.fn, getsem(("d", o.eng, o.slot, ep))))
                else:
                    st.append(("c", o.fn, getsem(("c", o.eng, o.tick[0])) if o.signals else None))
            st = streams["sp"]
            for (q, sl), p in last_dma.items():
                k, v, s = sig(p)
                if seen["sp"].get(k, (-1, 0)) < v:
                    st.append(("w", s, v[1]))
            self.nsems = len(semtab)

            def run(eng_obj, items):
                for it in items:
                    if it[0] == "w":
                        eng_obj.wait_ge(it[1], it[2])
                    elif it[0] == "d":
                        out, in_ = it[1]
                        eng_obj.dma_start(out=out, in_=in_).then_inc(it[2], 16)
                    else:
                        ins = it[1](eng_obj)
                        if it[2] is not None:
                            ins.then_inc(it[2], 1)

            with nc.Block() as block:
                @block.tensor
                def _(e):
                    run(e, streams["pe"])

                @block.scalar
                def _(e):
                    run(e, streams["act"])

                @block.vector
                def _(e):
                    run(e, streams["dve"])

                @block.gpsimd
                def _(e):
                    run(e, streams["pool"])

                @block.sync
                def _(e):
                    run(e, streams["sp"])
        d = {e: len(s) for e, s in streams.items()}
        d["sems"] = self.nsems
        d["ticks"] = dict(cnt)
        return d


import ml_dtypes

T = 2048
D = 2048
NT = 16
EPS = 1e-6
N_IN = 6945
GDN_Q0, GDN_K0, GDN_V0, GDN_Z0, GDN_B0, GDN_A0 = 0, 768, 1536, 2304, 3072, 3078
GLA_Q0, GLA_K0, GLA_V0, GLA_G0, GLA_LR0 = 3084, 3404, 3724, 4364, 5004
FOX_Q0, FOX_K0, FOX_V0, FOX_F0 = 5020, 5660, 6300, 6940
DFF = 8192

C_ID, C_MLE, C_MGT, C_NMGT, C_NMLT, C_ONES = 0, 128, 256, 384, 512, 640
C_MGT1 = 768
C_MLE1 = 768 + 129
NCF = 768 + 258
B_ID, B_MNEG, B_SEL = 0, 128, 256
NCB = 256 + 5 * 128
PC_NW1, PC_NW2, PC_CONVG, PC_CONVF, PC_BIASF = 0, 16, 32, 104, 488
NPC = 488 + 128
PR_NPM, PR_NPF, PR_GDNN, PR_GLAN, PR_ALOG, PR_DT, PR_FB = 0, 2048, 4096, 4224, 4352, 4448, 4544
NPR = 4544 + 80


class Rot:
    def __init__(self, items):
        self.items = list(items)
        self.i = 0

    def next(self):
        x = self.items[self.i % len(self.items)]
        self.i += 1
        return x


def make_consts():
    p = np.arange(128)[:, None]
    f = np.arange(128)[None, :]
    cf = np.zeros((128, NCF), np.float32)
    cf[:, C_ID:C_ID + 128] = (p == f)
    cf[:, C_MLE:C_MLE + 128] = (p <= f)
    cf[:, C_MGT:C_MGT + 128] = (p > f)
    cf[:, C_NMGT:C_NMGT + 128] = -1.0 * (p > f)
    cf[:, C_NMLT:C_NMLT + 128] = -1.0 * (p < f)
    cf[:, C_ONES:C_ONES + 128] = 1.0
    cf[:, C_MGT1:C_MGT1 + 128] = (p > f)
    cf[:, C_MGT1 + 128] = 1.0
    cf[:, C_MLE1:C_MLE1 + 128] = (p <= f)
    cf[:, C_MLE1 + 128] = 1.0
    cb = np.zeros((128, NCB), np.float32)
    cb[:, B_ID:B_ID + 128] = (p == f)
    cb[:, B_MNEG:B_MNEG + 128] = -30000.0 * (p > f)
    for h in range(5):
        for r in (h, 32 + h, 64 + h):
            cb[r, B_SEL + h * 128:B_SEL + (h + 1) * 128] = 1.0
    return cf, cb.astype(ml_dtypes.bfloat16)


def build(n_layers=2, debug=False, stop_after=None, parts=("gdn", "gla", "fox")):
    nc = bass.Bass("TRN2", target_bir_lowering=False)
    P = Prog(nc)
    x_d = P.dram("x", [T, D], F32, kind="ExternalInput", nparts=NT)
    w_in_d = P.dram("w_in", [2, D, N_IN], F32, kind="ExternalInput")
    w_out_d = P.dram("w_out", [2, D, D], F32, kind="ExternalInput")
    if stop_after not in ("mix", "norm"):
        w_up_d = P.dram("w_up", [2, D, 2 * DFF], F32, kind="ExternalInput")
        w_down_d = P.dram("w_down", [2, DFF, D], F32, kind="ExternalInput")
    pc_d = P.dram("pcols", [2, 128, NPC], F32, kind="ExternalInput")
    pr_d = P.dram("prows", [2, 128, NPR], F32, kind="ExternalInput")
    wg_d = P.dram("wg", [2, 17, 320], F32, kind="ExternalInput")
    cf_d = P.dram("cf", [128, NCF], F32, kind="ExternalInput")
    cb_d = P.dram("cb", [128, NCB], BF16, kind="ExternalInput")
    y_d = P.dram("y", [T, D], F32, kind="ExternalOutput", nparts=NT)
    dk = "ExternalOutput" if debug else "Internal"
    xa_d = P.dram("xa", [T, D], F32, kind=dk, nparts=NT)
    xb_d = P.dram("xb", [T, D], F32, kind="Internal", nparts=NT)
    oT_d = P.dram("oT", [16, 128, T], BF16, kind=dk, nparts=16)

    PS = [P.psum(f"ps{i}", [128, 512], F32) for i in range(8)]

    cf = P.sb("cf", [128, NCF], F32)
    cb = P.sb("cb", [128, NCB], BF16)
    P.dma("sp", cf[:], cf_d[:], writes=[cf])
    P.dma("sp", cb[:], cb_d[:], writes=[cb])
    ident = cf[:, C_ID:C_ID + 128]
    Mle = cf[:, C_MLE:C_MLE + 128]
    Mgt = cf[:, C_MGT:C_MGT + 128]
    nMgt = cf[:, C_NMGT:C_NMGT + 128]
    nMlt = cf[:, C_NMLT:C_NMLT + 128]
    ones = cf[:, C_ONES:C_ONES + 128]
    Mgt1 = cf[:, C_MGT1:C_MGT1 + 129]
    Mle1 = cf[:, C_MLE1:C_MLE1 + 129]
    identb = cb[:, B_ID:B_ID + 128]
    Mneg = cb[:, B_MNEG:B_MNEG + 128]

    def mm(out, lhsT, rhs, start, stop, reads, writes):
        P.op("pe", lambda e: e.matmul(out, lhsT, rhs, start=start, stop=stop), reads, writes)

    def tr(out, in_, reads, writes):
        P.op("pe", lambda e: e.transpose(out, in_, ident), list(reads) + [cf], writes)

    def act(out, in_, func, reads, writes, bias=None, scale=None, accum=None):
        kw = {}
        if bias is not None:
            kw["bias"] = bias
        if scale is not None:
            kw["scale"] = scale
        if accum is not None:
            kw["accum_out"] = accum
        P.op("act", lambda e: e.activation(out, in_, func, **kw), reads, writes)

    def ts(eng, out, in0, s1, s2, op0, op1, reads, writes):
        if s2 is None:
            P.op(eng, lambda e: e.tensor_scalar(out, in0, s1, None, op0), reads, writes)
        else:
            P.op(eng, lambda e: e.tensor_scalar(out, in0, s1, s2, op0, op1), reads, writes)

    def tt(eng, out, in0, in1, op, reads, writes):
        P.op(eng, lambda e: e.tensor_tensor(out, in0, in1, op), reads, writes)

    def stt(eng, out, in0, scalar, in1, op0, op1, reads, writes):
        P.op(eng, lambda e: e.scalar_tensor_tensor(out, in0, scalar, in1, op0, op1), reads, writes)

    def cp(eng, out, in_, reads, writes):
        if eng == "act":
            P.op("act", lambda e: e.copy(out, in_), reads, writes)
        else:
            P.op(eng, lambda e: e.tensor_copy(out, in_), reads, writes)

    def memset(eng, ap, v, writes):
        P.op(eng, lambda e: e.memset(ap, v), [], writes)

    def load_w(Wd, l, r0, nrt, c0, ncols, buf, bo=0):
        step = 4
        for a in range(0, nrt, step):
            n = min(step, nrt - a)
            src = Wd[l, r0 + a * 128:r0 + (a + n) * 128, c0:c0 + ncols].rearrange("(a p) c -> p a c", p=128)
            P.dma("pool", buf[:, a:a + n, bo:bo + ncols], src, writes=[buf])

    def rstd_cols(ss_ap, out_ap, tmp_ap, n, mean_scale, reads_writes):
        rw = reads_writes
        ts("dve", tmp_ap, ss_ap, mean_scale, EPS, ALU.mult, ALU.add, rw, rw)
        act(tmp_ap, tmp_ap, AF.Sqrt, rw, rw)
        P.op("dve", lambda e: e.reciprocal(out_ap, tmp_ap), rw, rw)

    def norm_phase(src, pcl, col0, hT, tok0, ntt):
        xts = [P.sb("xt", [128, D], F32) for _ in range(2)]
        junk = P.sb("junk", [128, D], F32)
        sts = [P.sb("nst", [128, 4], F32) for _ in range(2)]
        pr = Rot(PS[0:4])
        for i in range(ntt):
            tI = tok0 // 128 + i
            xt = xts[i % 2]
            s = sts[i % 2]
            P.dma("sp", xt[:], src[tI * 128:(tI + 1) * 128, :], reads=[src.part(tI)], writes=[xt])
            import os
            NS_ = int(os.environ.get("NORM_STEPS", "9"))
            memset("dve", s[:, 0:1], 0.0, [s])
            act(junk[:], xt[:], AF.Square, [xt, s], [junk, s], accum=s[:, 0:1])
            if NS_ < 2:
                continue
            rstd_cols(s[:, 0:1], s[:, 2:3], s[:, 1:2], 1, 1.0 / D, [s])
            if NS_ < 3:
                continue
            ts("dve", xt[:], xt[:], s[:, 2:3], None, ALU.mult, None, [xt, s], [xt])
            if NS_ < 4:
                continue
            for g in range(4):
                ps = pr.next()
                for j in range(4):
                    dt = g * 4 + j
                    tr(ps[:, j * 128:(j + 1) * 128], xt[:, dt * 128:(dt + 1) * 128], [xt], [ps])
                if NS_ < 5:
                    continue
                for j in range(4):
                    dt = g * 4 + j
                    o = hT[:, dt, i * 128:(i + 1) * 128]
                    sc = pcl[:, col0 + dt:col0 + dt + 1]
                    if j % 2 == 0:
                        act(o, ps[:, j * 128:(j + 1) * 128], AF.Copy, [ps, pcl], [hT.part(i)], scale=sc)
                    else:
                        ts("dve", o, ps[:, j * 128:(j + 1) * 128], sc, None, ALU.mult, None, [ps, pcl], [hT.part(i)])
        P.release(*xts, junk, *sts)

    def layer(l, src, dst):
        pcl = P.sb("pcl", [128, NPC], F32)
        P.dma("sp", pcl[:], pc_d[l], writes=[pcl])
        hT = P.sb("hT", [128, 16, T], BF16, nparts=NT)
        norm_phase(src, pcl, PC_NW1, hT, 0, NT)
        if stop_after == "norm":
            return
        prl = P.sb("prl", [128, NPR - 4096], F32)
        P.dma("sp", prl[:], pr_d[l, :, 4096:NPR], writes=[prl])
        R0 = 4096
        wb = Rot([P.sb("wb", [128, 16, 384], BF16) for _ in range(2)])
        prj = Rot(PS[4:8])

        def fm_proj(c0, M, consumer):
            wt = wb.next()
            load_w(w_in_d, l, 0, 16, c0, M, wt)
            for tb in range(4):
                ps = prj.next()
                for dt in range(16):
                    mm(ps[0:M, :], wt[:, dt, 0:M], hT[:, dt, tb * 512:(tb + 1) * 512], dt == 0, dt == 15, [wt, hT], [ps])
                consumer(tb, ps)

        def tm_proj(specs, consumer):
            wt = wb.next()
            o = 0
            for (c0, n) in specs:
                load_w(w_in_d, l, 0, 16, c0, n, wt, bo=o)
                o += n
            for tI in range(NT):
                ps = prj.next()
                for dt in range(16):
                    mm(ps[:, 0:o], hT[:, dt, tI * 128:(tI + 1) * 128], wt[:, dt, 0:o], dt == 0, dt == 15, [wt, hT], [ps])
                consumer(tI, ps)

        def finish_head(oc, z_ap, zreads, normrow_ap, oTh, c, st, tmp, psx):
            memset("dve", st[:, 0:1], 0.0, [st])
            act(tmp[:, 0:128], oc, AF.Square, [psx, st], [tmp, st], accum=st[:, 0:1])
            rstd_cols(st[:, 0:1], st[:, 2:3], st[:, 1:2], 1, 1.0 / 128, [st])
            stt("dve", tmp[:, 128:256], oc, st[:, 2:3], normrow_ap, ALU.mult, ALU.mult, [psx, st, prl], [tmp])
            act(tmp[:, 0:128], z_ap, AF.Silu, zreads, [tmp])
            tt("dve", tmp[:, 256:384], tmp[:, 128:256], tmp[:, 0:128], ALU.mult, [tmp], [tmp])

        if "gdn" in parts:
            ba = P.sb("ba", [128, NT, 12], F32)
            gall = P.sb("gall", [128, NT, 6], F32)
            beta = P.sb("beta", [128, NT, 6], F32)
            tm_proj([(GDN_B0, 12)], lambda tI, ps: cp("dve", ba[:, tI, :], ps[:, 0:12], [ps], [ba]))
            act(beta[:], ba[:, :, 0:6], AF.Sigmoid, [ba], [beta])
            for tI in range(NT):
                tt("dve", gall[:, tI, :], ba[:, tI, 6:12], prl[:, PR_DT - R0:PR_DT - R0 + 6], ALU.add, [ba, prl], [gall])
            act(gall[:], gall[:], AF.Softplus, [gall], [gall])
            ea = P.sb("ea", [128, 6], F32)
            act(ea[:], prl[:, PR_ALOG - R0:PR_ALOG - R0 + 6], AF.Exp, [prl], [ea])
            for tI in range(NT):
                stt("dve", gall[:, tI, :], gall[:, tI, :], -1.0, ea[:], ALU.mult, ALU.mult, [gall, ea], [gall])
            for h in range(6):
                zh = P.sb("zh", [128, NT, 128], BF16)
                tm_proj([(GDN_Z0 + h * 128, 128)], lambda tI, ps: cp("act", zh[:, tI, :], ps[:, 0:128], [ps], [zh]))
                raw = P.sb("raw", [128, T], F32)
                qkv = [P.sb("qkvc", [128, T], F32) for _ in range(3)]
                for qi, c0 in enumerate((GDN_Q0, GDN_K0, GDN_V0)):
                    fm_proj(c0 + h * 128, 128, lambda tb, ps: cp("act", raw[:, tb * 512:(tb + 1) * 512], ps[:, :], [ps], [raw]))
                    y = qkv[qi]
                    ct = qi * 6 + h
                    wc = lambda tap: pcl[:, PC_CONVG + ct * 4 + tap:PC_CONVG + ct * 4 + tap + 1]
                    ts("dve", y[:], raw[:], wc(3), None, ALU.mult, None, [raw, pcl], [y])
                    for sft in (1, 2, 3):
                        stt("dve", y[:, sft:], raw[:, 0:T - sft], wc(3 - sft), y[:, sft:], ALU.mult, ALU.add, [raw, pcl, y], [y])
                    act(y[:], y[:], AF.Silu, [y], [y])
                P.release(raw)
                qc, kc, vc = qkv
                u_st = P.sb("u_st", [128, NT, 128], F32, nparts=NT)
                wT_st = P.sb("wT_st", [128, NT, 128], BF16, nparts=NT)
                qgT_st = P.sb("qgT_st", [128, NT, 128], BF16, nparts=NT)
                qkT_st = P.sb("qkT_st", [128, NT, 128], BF16, nparts=NT)
                kend_st = P.sb("kend_st", [128, NT, 128], BF16, nparts=NT)
                egl_st = P.sb("egl_st", [128, NT], F32, nparts=NT)
                oTh = P.sb("oTh", [128, T], BF16)
                NS = 2
                slots = []
                for sI in range(NS):
                    slots.append(dict(
                        tok3=P.sb("tok3", [128, 384], F32), kbqg=P.sb("kbqg", [128, 256], F32),
                        tT=P.sb("tT", [128, 384], F32), rg=P.sb("rg", [128, 258], F32),
                        dm=P.sb("dm", [128, 384], F32), dmm=P.sb("dmm", [128, 384], F32),
                        pp=[P.sb("pp", [128, 256], F32) for _ in range(2)],
                        sol=[P.sb("sol", [128, 256], F32) for _ in range(2)],
                        cs=P.sb("cs", [128, 16], F32), junk=P.sb("jk", [128, 128], F32),
                        X=PS[sI * 2], Y=PS[sI * 2 + 1]))

                def stageA_steps(c, S):
                    tok3, kbqg, tT, rg, dm, dmm, cs, X, Y = S["tok3"], S["kbqg"], S["tT"], S["rg"], S["dm"], S["dmm"], S["cs"], S["X"], S["Y"]
                    sl = slice(c * 128, (c + 1) * 128)
                    tr(X[:, 0:128], qc[:, sl], [qc], [X])
                    tr(X[:, 128:256], kc[:, sl], [kc], [X])
                    tr(X[:, 256:384], vc[:, sl], [vc], [X])
                    memset("dve", cs[:, 0:2], 0.0, [cs])
                    act(S["junk"][:], X[:, 0:128], AF.Square, [X, cs], [S["junk"], cs], accum=cs[:, 0:1])
                    act(S["junk"][:], X[:, 128:256], AF.Square, [X, cs], [S["junk"], cs], accum=cs[:, 1:2])
                    rstd_cols(cs[:, 0:2], cs[:, 4:6], cs[:, 2:4], 2, 1.0, [cs])
                    ts("dve", tok3[:, 0:128], X[:, 0:128], cs[:, 4:5], 128.0 ** -0.5, ALU.mult, ALU.mult, [X, cs], [tok3])
                    act(tok3[:, 128:256], X[:, 128:256], AF.Copy, [X, cs], [tok3], scale=cs[:, 5:6])
                    cp("act", tok3[:, 256:384], X[:, 256:384], [X], [tok3])
                    yield
                    gcol = gall[:, c, h:h + 1]
                    bcol = beta[:, c, h:h + 1]
                    ts("dve", rg[:, 0:129], Mgt1, gcol, None, ALU.mult, None, [gall, cf], [rg])
                    ts("dve", rg[:, 129:258], Mle1, gcol, None, ALU.mult, None, [gall, cf], [rg])
                    mm(Y[:, 0:129], Mle, rg[:, 0:129], True, True, [rg, cf], [Y])
                    mm(Y[:, 256:385], Mgt, rg[:, 129:258], True, True, [rg, cf], [Y])
                    cp("dve", cs[:, 6:7], Y[:, 128:129], [Y], [cs])
                    act(cs[:, 7:8], Y[:, 128:129], AF.Exp, [Y], [cs])
                    act(cs[:, 8:9], Y[:, 384:385], AF.Exp, [Y], [cs])
                    tt("dve", cs[:, 9:10], cs[:, 6:7], Y[:, 384:385], ALU.add, [Y, cs], [cs])
                    act(egl_st[:, c:c + 1], cs[:, 9:10], AF.Exp, [cs], [egl_st.part(c)])
                    act(dm[:, 0:128], Y[:, 0:128], AF.Exp, [Y], [dm])
                    act(dm[:, 128:256], Y[:, 256:384], AF.Exp, [Y], [dm])
                    tt("pool", dmm[:, 0:128], dm[:, 0:128], nMgt, ALU.mult, [dm, cf], [dmm])
                    tt("pool", dmm[:, 128:256], dm[:, 128:256], nMlt, ALU.mult, [dm, cf], [dmm])
                    tt("pool", dmm[:, 256:384], dm[:, 128:256], Mle, ALU.mult, [dm, cf], [dmm])
                    yield
                    ts("dve", kbqg[:, 0:128], tok3[:, 128:256], bcol, None, ALU.mult, None, [tok3, beta], [kbqg])
                    ts("dve", kbqg[:, 128:256], tok3[:, 0:128], cs[:, 7:8], None, ALU.mult, None, [tok3, cs], [kbqg])
                    act(kend_st[:, c, :], tok3[:, 128:256], AF.Copy, [tok3, cs], [kend_st.part(c)], scale=cs[:, 8:9])
                    sol0 = S["sol"][0]
                    ts("dve", sol0[:, 0:128], tok3[:, 256:384], bcol, None, ALU.mult, None, [tok3, beta], [sol0])
                    ts("dve", sol0[:, 128:256], kbqg[:, 0:128], cs[:, 7:8], None, ALU.mult, None, [kbqg, cs], [sol0])
                    tr(X[:, 0:128], tok3[:, 128:256], [tok3], [X])
                    tr(X[:, 128:256], kbqg[:, 0:128], [kbqg], [X])
                    tr(X[:, 256:384], tok3[:, 0:128], [tok3], [X])
                    tr(X[:, 384:512], kbqg[:, 128:256], [kbqg], [X])
                    cp("act", tT[:, :], X[:, 0:384], [X], [tT])
                    cp("dve", qgT_st[:, c, :], X[:, 384:512], [X], [qgT_st.part(c)])
                    yield
                    mm(X[:, 0:128], tT[:, 128:256], tT[:, 0:128], True, True, [tT], [X])
                    mm(X[:, 128:256], tT[:, 0:128], tT[:, 128:256], True, True, [tT], [X])
                    mm(X[:, 256:384], tT[:, 0:128], tT[:, 256:384], True, True, [tT], [X])
                    pp0 = S["pp"][0]
                    tt("dve", pp0[:, 0:256], X[:, 0:256], dmm[:, 0:256], ALU.mult, [X, dmm], [pp0])
                    tt("dve", qkT_st[:, c, :], X[:, 256:384], dmm[:, 256:384], ALU.mult, [X, dmm], [qkT_st.part(c)])
                    yield
                    for k in range(7):
                        ppk = S["pp"][k % 2]
                        ppn = S["pp"][(k + 1) % 2]
                        solk = S["sol"][k % 2]
                        soln = S["sol"][(k + 1) % 2]
                        mm(Y[:, 0:256], ppk[:, 128:256], solk[:, :], True, True, [ppk, solk], [Y])
                        if k < 6:
                            mm(Y[:, 256:384], ppk[:, 128:256], ppk[:, 0:128], True, True, [ppk], [Y])
                            mm(Y[:, 384:512], ppk[:, 0:128], ppk[:, 128:256], True, True, [ppk], [Y])
                        tt("dve", soln[:, :], solk[:, :], Y[:, 0:256], ALU.add, [solk, Y], [soln])
                        if k < 6:
                            cp("act", ppn[:, :], Y[:, 256:512], [Y], [ppn])
                        yield
                    solf = S["sol"][1]
                    cp("pool", u_st[:, c, :], solf[:, 0:128], [solf], [u_st.part(c)])
                    tr(X[:, 0:128], solf[:, 128:256], [solf], [X])
                    cp("act", wT_st[:, c, :], X[:, 0:128], [X], [wT_st.part(c)])
                    yield

                for c0 in range(0, NT, NS):
                    gens = [stageA_steps(c0 + i, slots[i]) for i in range(NS)]
                    alive = True
                    while alive:
                        alive = False
                        for g in gens:
                            try:
                                next(g)
                                alive = True
                            except StopIteration:
                                pass
                for S in slots:
                    P.release(S["tok3"], S["kbqg"], S["tT"], S["rg"], S["dm"], S["dmm"], *S["pp"], *S["sol"], S["cs"], S["junk"])
                P.release(*qkv)
                Sf = P.sb("Sf", [128, 128], F32)
                Sb = P.sb("Sb", [128, 128], BF16)
                vnb = [P.sb("vnb", [128, 128], BF16) for _ in range(2)]
                fst = [P.sb("fst", [128, 4], F32) for _ in range(2)]
                ftmp = [P.sb("ftmp", [128, 384], F32) for _ in range(2)]
                memset("dve", Sf[:], 0.0, [Sf])
                memset("dve", Sb[:], 0.0, [Sb])
                A1, A2, A3, A4 = PS[4], PS[5], PS[6], PS[7]
                for c in range(NT):
                    vn = vnb[c % 2]
                    mm(A1[:, 0:128], wT_st[:, c, :], Sb[:], True, True, [wT_st.part(c), Sb], [A1])
                    tt("dve", vn[:], u_st[:, c, :], A1[:, 0:128], ALU.subtract, [u_st.part(c), A1], [vn])
                    mm(A2[:, 0:128], qgT_st[:, c, :], Sb[:], True, False, [qgT_st.part(c), Sb], [A2])
                    mm(A2[:, 0:128], qkT_st[:, c, :], vn[:], False, True, [qkT_st.part(c), vn], [A2])
                    mm(A3[:, 0:128], kend_st[:, c, :], vn[:], True, True, [kend_st.part(c), vn], [A3])
                    stt("dve", Sf[:], Sf[:], egl_st[:, c:c + 1], A3[:, 0:128], ALU.mult, ALU.add, [Sf, egl_st.part(c), A3], [Sf])
                    cp("act", Sb[:], Sf[:], [Sf], [Sb])
                    st, tmp = fst[c % 2], ftmp[c % 2]
                    finish_head(A2[:, 0:128], zh[:, c, :], [zh], prl[:, PR_GDNN - R0:PR_GDNN - R0 + 128], oTh, c, st, tmp, A2)
                    tr(A4[:, 0:128], tmp[:, 256:384], [tmp], [A4])
                    cp("act", oTh[:, c * 128:(c + 1) * 128], A4[:, 0:128], [A4], [oTh])
                P.dma("act", oT_d[h], oTh[:], reads=[oTh], writes=[oT_d.part(h)])
                P.release(Sf, Sb, *vnb, *fst, *ftmp, u_st, wT_st, qgT_st, qkT_st, kend_st, egl_st, oTh, zh)
            P.release(ba, gall, beta, ea)

        if "gla" in parts:
            lrT = P.sb("lrT", [17, T], F32)
            wga = P.sb("wga", [17, 320], F32)
            P.dma("sp", wga[:], wg_d[l], writes=[wga])
            memset("dve", lrT[:], 1.0, [lrT])
            fm_proj(GLA_LR0, 16, lambda tb, ps: cp("act", lrT[0:16, tb * 512:(tb + 1) * 512], ps[0:16, :], [ps], [lrT]))
            spg = P.sb("spg", [128, NT, 320], F32)
            for tI in range(NT):
                ps = prj.next()
                mm(ps[:, 0:320], lrT[0:17, tI * 128:(tI + 1) * 128], wga[0:17, :], True, True, [lrT, wga], [ps])
                act(spg[:, tI, :], ps[:, 0:320], AF.Softplus, [ps], [spg], scale=-1.0)
            P.release(lrT, wga)
            for h in range(5):
                qT = P.sb("glaqT", [64, T], F32)
                kT = P.sb("glakT", [64, T], F32)
                fm_proj(GLA_Q0 + h * 64, 64, lambda tb, ps: cp("act", qT[:, tb * 512:(tb + 1) * 512], ps[0:64, :], [ps], [qT]))
                fm_proj(GLA_K0 + h * 64, 64, lambda tb, ps: cp("act", kT[:, tb * 512:(tb + 1) * 512], ps[0:64, :], [ps], [kT]))
                kvg = P.sb("kvg", [128, NT, 320], BF16)
                tm_proj([(GLA_K0 + h * 64, 64), (GLA_V0 + h * 128, 128), (GLA_G0 + h * 128, 128)],
                        lambda tI, ps: cp("act", kvg[:, tI, :], ps[:, 0:320], [ps], [kvg]))
                oTh = P.sb("oTh", [128, T], BF16)
                Sf = P.sb("gSf", [64, 128], F32)
                Sb = P.sb("gSb", [64, 128], BF16)
                memset("dve", Sf[:], 0.0, [Sf])
                memset("dve", Sb[:], 0.0, [Sb])
                NB = 2
                eT = [P.sb("eT", [64, 256], F32) for _ in range(NB)]
                qk = [P.sb("qkd", [64, 256], BF16) for _ in range(NB)]
                kiv = [P.sb("kiv", [128, 64], BF16) for _ in range(NB)]
                etok = [P.sb("etok", [128, 64], F32) for _ in range(NB)]
                att = [P.sb("att", [128, 128], BF16) for _ in range(NB)]
                fst = [P.sb("fst", [128, 4], F32) for _ in range(NB)]
                ftmp = [P.sb("ftmp", [128, 384], F32) for _ in range(NB)]
                s1 = [P.sb("gs1", [64, 128], F32) for _ in range(NB)]
                B1, B2, B3, B4 = PS[0], PS[1], PS[2], PS[3]
                for c in range(NT):
                    i2 = c % NB
                    sl = slice(c * 128, (c + 1) * 128)
                    sph = spg[:, c, h * 64:(h + 1) * 64]
                    mm(B1[:, 0:64], Mle, sph, True, True, [cf, spg], [B1])
                    mm(B1[0:64, 128:256], sph, Mle, True, True, [cf, spg], [B1])
                    act(eT[i2][:, 0:128], B1[0:64, 128:256], AF.Exp, [B1], [eT[i2]], scale=-1.0 / 16)
                    act(eT[i2][:, 128:256], B1[0:64, 128:256], AF.Exp, [B1], [eT[i2]], scale=1.0 / 16)
                    act(etok[i2][:], B1[:, 0:64], AF.Exp, [B1], [etok[i2]], scale=1.0 / 16)
                    stt("dve", qk[i2][:, 0:128], qT[:, sl], 0.125, eT[i2][:, 0:128], ALU.mult, ALU.mult, [qT, eT[i2]], [qk[i2]])
                    tt("dve", qk[i2][:, 128:256], kT[:, sl], eT[i2][:, 128:256], ALU.mult, [kT, eT[i2]], [qk[i2]])
                    tt("dve", kiv[i2][:], kvg[:, c, 0:64], etok[i2][:], ALU.mult, [kvg, etok[i2]], [kiv[i2]])
                    mm(B2[:, 0:128], qk[i2][:, 128:256], qk[i2][:, 0:128], True, True, [qk[i2]], [B2])
                    tt("dve", att[i2][:], B2[:, 0:128], Mle, ALU.mult, [B2, cf], [att[i2]])
                    mm(B3[:, 0:128], qk[i2][:, 0:128], Sb[:], True, False, [qk[i2], Sb], [B3])
                    mm(B3[:, 0:128], att[i2][:], kvg[:, c, 64:192], False, True, [att[i2], kvg], [B3])
                    mm(B2[0:64, 128:256], kiv[i2][:], kvg[:, c, 64:192], True, True, [kiv[i2], kvg], [B2])
                    ts("dve", s1[i2][:], Sf[:], eT[i2][:, 127:128], None, ALU.mult, None, [Sf, eT[i2]], [s1[i2]])
                    stt("dve", Sf[:], B2[0:64, 128:256], eT[i2][:, 127:128], s1[i2][:], ALU.mult, ALU.add, [B2, eT[i2], s1[i2]], [Sf])
                    cp("act", Sb[:], Sf[:], [Sf], [Sb])
                    st, tmp = fst[i2], ftmp[i2]
                    finish_head(B3[:, 0:128], kvg[:, c, 192:320], [kvg], prl[:, PR_GLAN - R0:PR_GLAN - R0 + 128], oTh, c, st, tmp, B3)
                    tr(B4[:, 0:128], tmp[:, 256:384], [tmp], [B4])
                    cp("act", oTh[:, sl], B4[:, 0:128], [B4], [oTh])
                P.dma("act", oT_d[6 + h], oTh[:], reads=[oTh], writes=[oT_d.part(6 + h)])
                P.release(qT, kT, kvg, oTh, Sf, Sb, *eT, *qk, *kiv, *etok, *att, *fst, *ftmp, *s1)
            P.release(spg)

        if "fox" in parts:
            vsb = P.sb("vsb", [128, NT, 5, 129], BF16)
            fsb = P.sb("fsb", [128, NT, 5], F32)
            memset("dve", vsb[:], 1.0, [vsb])

            def cons_v1(tI, ps):
                cp("act", vsb[:, tI, 0:4, 0:128], ps[:, 0:512].rearrange("p (h d) -> p h d", h=4), [ps], [vsb])

            def cons_v2(tI, ps):
                cp("act", vsb[:, tI, 4, 0:128], ps[:, 0:128], [ps], [vsb])
                cp("dve", fsb[:, tI, :], ps[:, 128:133], [ps], [fsb])
            tm_proj([(FOX_V0, 384)], lambda tI, ps: cp("act", vsb[:, tI, 0:3, 0:128], ps[:, 0:384].rearrange("p (h d) -> p h d", h=3), [ps], [vsb]))
            tm_proj([(FOX_V0 + 384, 256), (FOX_F0, 5)], lambda tI, ps: (
                cp("act", vsb[:, tI, 3:5, 0:128], ps[:, 0:256].rearrange("p (h d) -> p h d", h=2), [ps], [vsb]),
                cp("dve", fsb[:, tI, :], ps[:, 256:261], [ps], [fsb])))
            for tI in range(NT):
                tt("dve", fsb[:, tI, :], fsb[:, tI, :], prl[:, PR_FB - R0:PR_FB - R0 + 5], ALU.add, [fsb, prl], [fsb])
            act(fsb[:], fsb[:], AF.Softplus, [fsb], [fsb], scale=-1.0)
            cpos = P.sb("cpos", [128, NT, 5], F32)
            offc = P.sb("offc", [128, NT, 5], F32)
            totc = P.sb("totc", [128, NT, 5], F32)
            C1 = PS[0]
            fs2 = fsb[:].rearrange("p t h -> p (t h)")
            mm(C1[:, 0:80], Mle, fs2, True, True, [cf, fsb], [C1])
            mm(C1[:, 128:208], ones, fs2, True, True, [cf, fsb], [C1])
            cp("dve", totc[:].rearrange("p t h -> p (t h)"), C1[:, 128:208], [C1], [totc])
            memset("dve", offc[:, 0, :], 0.0, [offc])
            for tI in range(1, NT):
                tt("dve", offc[:, tI, :], offc[:, tI - 1, :], totc[:, tI - 1, :], ALU.add, [offc, totc], [offc])
            tt("dve", cpos[:].rearrange("p t h -> p (t h)"), C1[:, 0:80], offc[:].rearrange("p t h -> p (t h)"), ALU.add, [C1, offc], [cpos])
            crow = P.sb("crow", [5, T], F32)
            totr = P.sb("totr", [5, NT], F32)
            offr = P.sb("offr", [5, NT], F32)
            for tI in range(NT):
                ps = PS[1 + tI % 2]
                mm(ps[0:5, 0:129], fsb[:, tI, :], Mle1, True, True, [cf, fsb], [ps])
                cp("act", crow[:, tI * 128:(tI + 1) * 128], ps[0:5, 0:128], [ps], [crow])
                cp("dve", totr[:, tI:tI + 1], ps[0:5, 128:129], [ps], [totr])
            memset("dve", offr[:, 0:1], 0.0, [offr])
            for tI in range(1, NT):
                tt("dve", offr[:, tI:tI + 1], offr[:, tI - 1:tI], totr[:, tI - 1:tI], ALU.add, [offr, totr], [offr])
            relc = P.sb("relc", [5, NT], F32)
            for qb in range(4):
                ts("dve", relc[:, qb * 4:(qb + 1) * 4], offr[:, qb * 4:(qb + 1) * 4], offr[:, qb * 4:qb * 4 + 1], None, ALU.subtract, None, [offr], [relc])
            for tI in range(NT):
                ts("dve", crow[:, tI * 128:(tI + 1) * 128], crow[:, tI * 128:(tI + 1) * 128], relc[:, tI:tI + 1], -1.0, ALU.add, ALU.mult, [crow, relc], [crow])
            R3 = P.sb("R3", [69, T], BF16)
            rres = P.sb("rres", [5, T], F32)
            rtmp = P.sb("rtmp", [5, T], F32)
            rb = [P.sb("rb", [5, T], BF16) for _ in range(2)]
            memset("dve", R3[:], 0.0, [R3])
            cp("dve", R3[0:5, :], crow[:, :], [crow], [R3])
            cp("dve", rtmp[:], R3[0:5, :], [R3], [rtmp])
            tt("dve", rres[:], crow[:], rtmp[:], ALU.subtract, [crow, rtmp], [rres])
            cp("dve", rb[0][:], rres[:], [rres], [rb[0]])
            cp("dve", rtmp[:], rb[0][:], [rb[0]], [rtmp])
            tt("dve", rres[:], rres[:], rtmp[:], ALU.subtract, [rres, rtmp], [rres])
            cp("dve", rb[1][:], rres[:], [rres], [rb[1]])
            P.dma("sp", R3[32:37, :], rb[0][:], reads=[rb[0]], writes=[R3])
            P.dma("sp", R3[64:69, :], rb[1][:], reads=[rb[1]], writes=[R3])
            P.release(*rb)
            P.release(crow, totr, offr, relc, rres, rtmp, totc)
            for h in range(5):
                qT = P.sb("fqT", [128, T], BF16)
                kT = P.sb("fkT", [128, T], BF16)
                fm_proj(FOX_Q0 + h * 128, 128, lambda tb, ps: act(qT[:, tb * 512:(tb + 1) * 512], ps[:, :], AF.Copy, [ps], [qT], scale=128.0 ** -0.5))
                fm_proj(FOX_K0 + h * 128, 128, lambda tb, ps: cp("dve", kT[:, tb * 512:(tb + 1) * 512], ps[:, :], [ps], [kT]))
                btab = P.sb("btab", [128, NT, 4], F32)
                for qb in range(4):
                    ts("dve", btab[:, :, qb], cpos[:, :, h], offc[:, 4 * qb, h:h + 1], None, ALU.subtract, None, [cpos, offc], [btab])
                oTh = P.sb("oTh", [128, T], BF16)
                pts = [P.sb("pt", [128, 512], BF16) for _ in range(3)]
                osb = [P.sb("osb", [128, 132], F32) for _ in range(2)]
                sc = Rot(PS[0:2])
                selh = cb[0:69, B_SEL + h * 128:B_SEL + (h + 1) * 128]
                for qb in range(4):
                    accs = [PS[2], PS[3], PS[4], PS[5]]
                    accap = lambda j: accs[j][:, 0:129]
                    for tk in range(4 * qb + 4):
                        j0 = max(0, tk - 4 * qb)
                        q0 = j0 * 128
                        ps = sc.next()
                        diag = tk >= 4 * qb
                        mm(ps[:, q0:512], kT[:, tk * 128:(tk + 1) * 128], qT[:, qb * 512 + q0:(qb + 1) * 512], True, False, [kT, qT], [ps])
                        mm(ps[:, q0:512], selh, R3[0:69, qb * 512 + q0:(qb + 1) * 512], False, not diag, [cb, R3], [ps])
                        if diag:
                            mm(ps[:, q0:q0 + 128], identb, Mneg, False, True, [cb], [ps])
                        pt = pts[tk % 3]
                        act(pt[:, q0:512], ps[:, q0:512], AF.Exp, [ps, btab], [pt], bias=btab[:, tk, qb:qb + 1])
                        for j in range(j0, 4):
                            tq = 4 * qb + j
                            mm(accap(j), pt[:, j * 128:(j + 1) * 128], vsb[:, tk, h, :], tk == 0, tk == tq, [pt, vsb], [accs[j]])
                    for j in range(4):
                        tq = 4 * qb + j
                        ob = osb[j % 2]
                        a = accap(j)
                        P.op("dve", lambda e, ob=ob, a=a: e.reciprocal(ob[:, 128:129], a[:, 128:129]), [accs[j]], [ob])
                        ts("dve", ob[:, 0:128], a[:, 0:128], ob[:, 128:129], None, ALU.mult, None, [accs[j], ob], [ob])
                        pst = PS[6 + j % 2]
                        tr(pst[:, 0:128], ob[:, 0:128], [ob], [pst])
                        cp("act", oTh[:, tq * 128:(tq + 1) * 128], pst[:, 0:128], [pst], [oTh])
                P.dma("act", oT_d[11 + h], oTh[:], reads=[oTh], writes=[oT_d.part(11 + h)])
                P.release(qT, kT, btab, oTh, *pts, *osb)
            P.release(vsb, fsb, cpos, offc, R3)
        P.release(prl, *wb.items, hT)

        oT = P.sb("oTall", [128, 16, T], BF16)
        for ct in range(16):
            P.dma("sp", oT[:, ct, :], oT_d[ct], reads=[oT_d.part(ct)], writes=[oT])
        wo = P.sb("wo", [128, 16, D], BF16)
        for cbk in range(4):
            load_w(w_out_d, l, 0, 16, cbk * 512, 512, wo, bo=cbk * 512)
        nrow = P.sb("nrow", [128, D], F32)
        P.dma("sp", nrow[:], pr_d[l, :, PR_NPM:PR_NPM + D], writes=[nrow])
        xts = [P.sb("xt", [128, D], F32) for _ in range(2)]
        yts = [P.sb("yt", [128, D], F32) for _ in range(2)]
        sts = [P.sb("st", [128, 8], F32) for _ in range(2)]
        junk = P.sb("junk", [128, 512], F32)
        pr = Rot(PS)
        for tI in range(NT):
            xt, yt, st = xts[tI % 2], yts[tI % 2], sts[tI % 2]
            P.dma("sp", xt[:], src[tI * 128:(tI + 1) * 128, :], reads=[src.part(tI)], writes=[xt])
            import os
            OS_ = int(os.environ.get("OP_STEPS", "9"))
            memset("dve", st[:, 0:4], 0.0, [st])
            for cbk in range(4):
                if OS_ < 2:
                    continue
                ps = pr.next()
                for ct in range(16):
                    mm(ps[:, :], oT[:, ct, tI * 128:(tI + 1) * 128], wo[:, ct, cbk * 512:(cbk + 1) * 512], ct == 0, ct == 15, [oT, wo], [ps])
                if OS_ < 3:
                    continue
                act(junk[:], ps[:, :], AF.Square, [ps, st], [junk, st], accum=st[:, cbk:cbk + 1])
                tt("dve", yt[:, cbk * 512:(cbk + 1) * 512], ps[:, :], nrow[:, cbk * 512:(cbk + 1) * 512], ALU.mult, [ps, nrow], [yt])
            if OS_ >= 4:
                P.op("dve", lambda e, st=st: e.tensor_reduce(st[:, 4:5], st[:, 0:4], AX.X, ALU.add), [st], [st])
                rstd_cols(st[:, 4:5], st[:, 6:7], st[:, 5:6], 1, 1.0 / D, [st])
                stt("dve", xt[:], yt[:], st[:, 6:7], xt[:], ALU.mult, ALU.add, [yt, st, xt], [xt])
            P.dma("act", xa_d[tI * 128:(tI + 1) * 128, :], xt[:], reads=[xt], writes=[xa_d.part(tI)])
        P.release(oT, wo, nrow, *xts, *yts, *sts, junk)
        if stop_after == "mix":
            return

        TB = 1024
        NG = 4
        nrow = P.sb("nrow2", [128, D], F32)
        P.dma("sp", nrow[:], pr_d[l, :, PR_NPF:PR_NPF + D], writes=[nrow])
        carry = P.sb("carry", [128, 128, 2], F32)
        for blk in range(T // TB):
            tok0 = blk * TB
            ntt = TB // 128
            h2T = P.sb("h2T", [128, 16, TB], BF16, nparts=ntt)
            norm_phase(xa_d, pcl, PC_NW2, h2T, tok0, ntt)
            ysb = P.sb("ysb", [128, ntt, D], F32, nparts=ntt)
            wup = Rot([P.sb("wup", [128, 16, 256], BF16) for _ in range(2)])
            wdn = Rot([P.sb("wdn", [128, NG, 512], BF16) for _ in range(3)])
            aT = Rot([P.sb("aT", [128, NG, TB], BF16) for _ in range(2)])
            ug = [P.sb("ug", [128, 2, TB + 2], F32) for _ in range(2)]
            cv = [P.sb("cv", [128, 2, TB], F32) for _ in range(2)]
            upr = Rot(PS[0:4])
            dpr = Rot(PS[4:8])
            for fg in range(64 // NG):
                a_t = aT.next()
                for fi in range(NG):
                    f = fg * NG + fi
                    wt = wup.next()
                    load_w(w_up_d, l, 0, 16, f * 128, 128, wt, bo=0)
                    load_w(w_up_d, l, 0, 16, DFF + f * 128, 128, wt, bo=128)
                    u = ug[f % 2]
                    c = cv[f % 2]
                    for gv in range(2):
                        for tb in range(TB // 512):
                            ps = upr.next()
                            for dt in range(16):
                                mm(ps[:, :], wt[:, dt, gv * 128:(gv + 1) * 128], h2T[:, dt, tb * 512:(tb + 1) * 512], dt == 0, dt == 15, [wt, h2T], [ps])
                            cp("act", u[:, gv, 2 + tb * 512:2 + (tb + 1) * 512], ps[:, :], [ps], [u])
                        ft = f + 64 * gv
                        if blk == 0:
                            memset("dve", u[:, gv, 0:2], 0.0, [u])
                        else:
                            cp("dve", u[:, gv, 0:2], carry[:, ft, :], [carry], [u])
                    wcf = lambda ft, tap: pcl[:, PC_CONVF + ft * 3 + tap:PC_CONVF + ft * 3 + tap + 1]
                    for gv in range(2):
                        ft = f + 64 * gv
                        eng = "dve"
                        ts(eng, c[:, gv, :], u[:, gv, 2:TB + 2], wcf(ft, 2), pcl[:, PC_BIASF + ft:PC_BIASF + ft + 1], ALU.mult, ALU.add, [u, pcl], [c])
                        stt(eng, c[:, gv, :], u[:, gv, 1:TB + 1], wcf(ft, 1), c[:, gv, :], ALU.mult, ALU.add, [u, pcl, c], [c])
                        stt(eng, c[:, gv, :], u[:, gv, 0:TB], wcf(ft, 0), c[:, gv, :], ALU.mult, ALU.add, [u, pcl, c], [c])
                        if blk < T // TB - 1:
                            cp("dve", carry[:, ft, :], u[:, gv, TB:TB + 2], [u], [carry])
                    act(c[:, 0, :], c[:, 0, :], AF.Gelu_apprx_tanh, [c], [c])
                    tt("dve", a_t[:, fi, :], c[:, 0, :], c[:, 1, :], ALU.mult, [c], [a_t])
                for cbk in range(4):
                    wd = wdn.next()
                    load_w(w_down_d, l, fg * NG * 128, NG, cbk * 512, 512, wd)
                    for tI in range(ntt):
                        ps = dpr.next()
                        for fi in range(NG):
                            mm(ps[:, :], a_t[:, fi, tI * 128:(tI + 1) * 128], wd[:, fi, :], fi == 0, fi == NG - 1, [a_t, wd], [ps])
                        o = ysb[:, tI, cbk * 512:(cbk + 1) * 512]
                        if fg == 0:
                            cp("act", o, ps[:, :], [ps], [ysb.part(tI)])
                        else:
                            tt("dve", o, o, ps[:, :], ALU.add, [ps, ysb.part(tI)], [ysb.part(tI)])
            P.release(*wup.items, *wdn.items, *aT.items, *ug, *cv, h2T)
            xts = [P.sb("xt", [128, D], F32) for _ in range(2)]
            sts = [P.sb("st", [128, 4], F32) for _ in range(2)]
            junk = P.sb("junk", [128, D], F32)
            for i in range(ntt):
                tI = tok0 // 128 + i
                xt, st = xts[i % 2], sts[i % 2]
                P.dma("sp", xt[:], xa_d[tI * 128:(tI + 1) * 128, :], reads=[xa_d.part(tI)], writes=[xt])
                memset("dve", st[:, 0:1], 0.0, [st])
                act(junk[:], ysb[:, i, :], AF.Square, [ysb.part(i), st], [junk, st], accum=st[:, 0:1])
                rstd_cols(st[:, 0:1], st[:, 2:3], st[:, 1:2], 1, 1.0 / D, [st])
                tt("pool", ysb[:, i, :], ysb[:, i, :], nrow[:], ALU.mult, [ysb.part(i), nrow], [ysb.part(i)])
                stt("dve", xt[:], ysb[:, i, :], st[:, 2:3], xt[:], ALU.mult, ALU.add, [ysb.part(i), st, xt], [xt])
                P.dma("act", dst[tI * 128:(tI + 1) * 128, :], xt[:], reads=[xt], writes=[dst.part(tI)])
            P.release(*xts, *sts, junk, ysb)
            if blk == T // TB - 1:
                P.release(carry)
        P.release(nrow, pcl)

    for l in range(n_layers):
        src = x_d if l == 0 else xb_d
        dst = xb_d if l < n_layers - 1 else y_d
        layer(l, src, dst)
        if stop_after is not None:
            break
    counts = P.emit()
    return nc, counts


def prep_params(inp):
    f32 = np.float32
    pc = np.zeros((2, 128, NPC), f32)
    pr = np.zeros((2, 128, NPR), f32)
    wg = np.zeros((2, 17, 320), f32)
    for l in range(2):
        pc[l, :, PC_NW1:PC_NW1 + 16] = inp["norm_pre_mix"][l].reshape(16, 128).T
        pc[l, :, PC_NW2:PC_NW2 + 16] = inp["norm_pre_ffn"][l].reshape(16, 128).T
        cg = inp["conv_gdn"][l]
        pc[l, :, PC_CONVG:PC_CONVG + 72] = cg.reshape(4, 18, 128).transpose(2, 1, 0).reshape(128, 72)
        cfw = inp["conv_ffn"][l]
        pc[l, :, PC_CONVF:PC_CONVF + 384] = cfw.reshape(3, 128, 128).transpose(2, 1, 0).reshape(128, 384)
        pc[l, :, PC_BIASF:PC_BIASF + 128] = inp["conv_ffn_bias"][l].reshape(128, 128).T
        pr[l, :, PR_NPM:PR_NPM + 2048] = inp["norm_post_mix"][l][None, :]
        pr[l, :, PR_NPF:PR_NPF + 2048] = inp["norm_post_ffn"][l][None, :]
        pr[l, :, PR_GDNN:PR_GDNN + 128] = inp["gdn_norm"][l][None, :]
        pr[l, :, PR_GLAN:PR_GLAN + 128] = inp["gla_norm"][l][None, :]
        pr[l, :, PR_ALOG:PR_ALOG + 6] = inp["gdn_a_log"][l][None, :]
        pr[l, :, PR_DT:PR_DT + 6] = inp["gdn_dt_bias"][l][None, :]
        pr[l, :, PR_FB:PR_FB + 5] = inp["fox_f_bias"][l][None, :]
        wg[l, 0:16] = inp["gla_w_gate"][l]
        wg[l, 16] = inp["gla_b_gate"][l]
    return pc, pr, wg


_CACHE = {}


def kernel(**inputs):
    inp = {k: np.asarray(v) for k, v in inputs.items()}
    if "nc" not in _CACHE:
        _CACHE["nc"] = build(2)[0]
    nc = _CACHE["nc"]
    pc, pr, wg = prep_params(inp)
    cf, cb = make_consts()
    x = np.ascontiguousarray(inp["x"], dtype=np.float32)
    shared = {"w_in": np.ascontiguousarray(inp["w_in"], dtype=np.float32),
              "w_out": np.ascontiguousarray(inp["w_out"], dtype=np.float32),
              "w_up": np.ascontiguousarray(inp["w_up"], dtype=np.float32),
              "w_down": np.ascontiguousarray(inp["w_down"], dtype=np.float32),
              "pcols": pc, "prows": pr, "wg": wg, "cf": cf, "cb": cb}
    in_maps = [dict(shared, x=x[b]) for b in range(8)]
    res = run_bass_kernel_spmd(nc, in_maps, core_ids=list(range(8)))
    return np.stack([np.asarray(r["y"], dtype=np.float32) for r in res.results], axis=0)
```

```python
import numpy as np
import concourse.bass as bass
import concourse.mybir as mybir
from concourse.bass_utils import run_bass_kernel_spmd

F32 = mybir.dt.float32
BF16 = mybir.dt.bfloat16
ALU = mybir.AluOpType
AF = mybir.ActivationFunctionType
AX = mybir.AxisListType

SBUF_BASE = 16512 + 2048
SBUF_END = 229376


class Dep:
    __slots__ = ("lw", "rd", "name", "excl")

    def __init__(self, name=""):
        self.lw = None
        self.rd = []
        self.name = name
        self.excl = False


class Buf:
    def __init__(self, name, handle, nparts=1, off=None, size=None):
        self.name = name
        self.h = handle
        self.parts = [Dep(f"{name}.{i}") for i in range(nparts)]
        self.off = off
        self.size = size

    def __getitem__(self, key):
        return self.h[key]

    def part(self, i):
        return self.parts[i]

    @property
    def all(self):
        return self.parts


class Op:
    __slots__ = ("eng", "fn", "waits", "is_dma", "slot", "seq", "signals", "tick", "idx")


class Prog:
    ENGS = ("pe", "act", "dve", "pool", "sp")
    NSLOT = {"sp": 8, "pool": 8, "act": 6}

    def __init__(self, nc):
        self.nc = nc
        self.ops = []
        self.dma_count = {q: 0 for q in self.NSLOT}
        self.free = [(SBUF_BASE, SBUF_END - SBUF_BASE)]
        self.retired = []
        self.nbuf = 0
        self.psum_banks = None

    def sb(self, name, shape, dtype, nparts=1):
        esz = 4 if dtype == F32 else 2
        n = 1
        for s in shape[1:]:
            n *= s
        size = (n * esz + 63) // 64 * 64
        for i, (o, s) in enumerate(self.free):
            if s >= size:
                off = o
                if s == size:
                    self.free.pop(i)
                else:
                    self.free[i] = (o + size, s - size)
                break
        else:
            raise RuntimeError(f"SBUF OOM allocating {name} {shape} size {size}; free={self.free}")
        self.nbuf += 1
        h = self.nc.alloc_sbuf_tensor_at(f"{name}_{self.nbuf}", list(shape), dtype, offset=off)
        b = Buf(name, h, nparts, off, size)
        keep = []
        for (ro, rs, deps) in self.retired:
            if ro < off + size and off < ro + rs:
                for d in deps:
                    for p in b.parts:
                        if d.lw is not None:
                            p.rd.append(d.lw)
                        p.rd.extend(d.rd)
                if not (off <= ro and ro + rs <= off + size):
                    keep.append((ro, rs, deps))
            else:
                keep.append((ro, rs, deps))
        self.retired = keep
        return b

    def release(self, *bufs):
        for b in bufs:
            self.retired.append((b.off, b.size, list(b.parts)))
            self.free.append((b.off, b.size))
        self.free.sort()
        m = []
        for o, s in self.free:
            if m and m[-1][0] + m[-1][1] == o:
                m[-1] = (m[-1][0], m[-1][1] + s)
            else:
                m.append((o, s))
        self.free = m

    def dram(self, name, shape, dtype, kind="Internal", nparts=1):
        h = self.nc.dram_tensor(name, list(shape), dtype, kind=kind)
        return Buf(name, h, nparts)

    def psum(self, name, shape, dtype=F32, nparts=1):
        h = self.nc.alloc_psum_tensor(name, list(shape), dtype)
        b = Buf(name, h, nparts)
        for p in b.parts:
            p.excl = True
        return b

    def _deps(self, reads, writes):
        raw, war = set(), set()
        for d in reads:
            if d.lw is not None:
                raw.add(d.lw)
        for d in writes:
            if d.lw is not None:
                war.add(d.lw)
            for r in d.rd:
                war.add(r)
        return raw, war

    def _flat(self, lst):
        out = []
        for x in lst:
            if isinstance(x, Buf):
                out.extend(x.parts)
            elif isinstance(x, Dep):
                out.append(x)
            else:
                out.extend(self._flat(x))
        return out

    def op(self, eng, fn, reads=(), writes=()):
        reads = self._flat(reads)
        writes = self._flat(writes)
        ex = [d for d in reads if d.excl]
        if ex:
            reads = [d for d in reads if not d.excl]
            writes = writes + [d for d in ex if d not in writes]
        o = Op()
        o.eng = eng
        o.fn = fn
        o.is_dma = False
        o.signals = False
        o.tick = None
        o.idx = len(self.ops)
        raw, war = self._deps(reads, writes)
        waits = []
        for p in raw:
            if p.is_dma or p.eng != eng or eng != "pe":
                waits.append(p)
        for p in war:
            if p in raw:
                continue
            if p.is_dma or p.eng != eng:
                waits.append(p)
        o.waits = waits
        for p in waits:
            if not p.is_dma:
                p.signals = True
        for d in reads:
            d.rd.append(o)
        for d in writes:
            d.lw = o
            d.rd = []
        self.ops.append(o)
        return o

    def dma(self, q, out, in_, reads=(), writes=()):
        reads = self._flat(reads)
        writes = self._flat(writes)
        o = Op()
        o.eng = q
        o.fn = (out, in_)
        o.is_dma = True
        o.signals = False
        o.tick = None
        o.idx = len(self.ops)
        n = self.dma_count[q]
        self.dma_count[q] += 1
        o.slot = n % self.NSLOT[q]
        o.seq = n // self.NSLOT[q]
        raw, war = self._deps(reads, writes)
        waits = list(raw | war)
        o.waits = waits
        for p in waits:
            if not p.is_dma:
                p.signals = True
        for d in reads:
            d.rd.append(o)
        for d in writes:
            d.lw = o
            d.rd = []
        self.ops.append(o)
        return o

    CLIM = 4000
    DLIM = 250

    def emit(self):
        nc = self.nc
        import contextlib
        es = contextlib.ExitStack()
        with es:
            semtab = {}

            def getsem(key):
                if key not in semtab:
                    semtab[key] = es.enter_context(nc.semaphore("s_" + "_".join(str(k) for k in key)))
                return semtab[key]

            cnt = {e: 0 for e in ("pe", "act", "dve", "pool")}
            for o in self.ops:
                if not o.is_dma and o.signals:
                    c = cnt[o.eng]
                    cnt[o.eng] += 1
                    o.tick = (c // self.CLIM, c % self.CLIM + 1)
            streams = {e: [] for e in self.ENGS}
            seen = {e: {} for e in self.ENGS}

            def sig(p):
                if p.is_dma:
                    ep, sq = p.seq // self.DLIM, p.seq % self.DLIM
                    return ("d", p.eng, p.slot), (ep, 16 * (sq + 1)), getsem(("d", p.eng, p.slot, ep))
                return ("c", p.eng), p.tick, getsem(("c", p.eng, p.tick[0]))

            last_dma = {}
            for o in self.ops:
                st = streams[o.eng]
                sn = seen[o.eng]
                best = {}
                plist = list(o.waits)
                if o.is_dma:
                    prev = last_dma.get((o.eng, o.slot))
                    if prev is not None:
                        plist.append(prev)
                    last_dma[(o.eng, o.slot)] = o
                for p in plist:
                    k, v, s = sig(p)
                    if sn.get(k, (-1, 0)) >= v:
                        continue
                    if k not in best or best[k][0] < v:
                        best[k] = (v, s)
                for k, (v, s) in best.items():
                    sn[k] = v
                    st.append(("w", s, v[1]))
                if o.is_dma:
                    ep = o.seq // self.DLIM
                    st.append(("d", o.fn, getsem(("d", o.eng, o.slot, ep))))
                else:
                    st.append(("c", o.fn, getsem(("c", o.eng, o.tick[0])) if o.signals else None))
            st = streams["sp"]
            for (q, sl), p in last_dma.items():
                k, v, s = sig(p)
                if seen["sp"].get(k, (-1, 0)) < v:
                    st.append(("w", s, v[1]))
            self.nsems = len(semtab)

            def run(eng_obj, items):
                for it in items:
                    if it[0] == "w":
                        eng_obj.wait_ge(it[1], it[2])
                    elif it[0] == "d":
                        out, in_ = it[1]
                        eng_obj.dma_start(out=out, in_=in_).then_inc(it[2], 16)
                    else:
                        ins = it[1](eng_obj)
                        if it[2] is not None:
                            ins.then_inc(it[2], 1)

            with nc.Block() as block:
                @block.tensor
                def _(e):
                    run(e, streams["pe"])

                @block.scalar
                def _(e):
                    run(e, streams["act"])

                @block.vector
                def _(e):
                    run(e, streams["dve"])

                @block.gpsimd
                def _(e):
                    run(e, streams["pool"])

                @block.sync
                def _(e):
                    run(e, streams["sp"])
        d = {e: len(s) for e, s in streams.items()}
        d["sems"] = self.nsems
        d["ticks"] = dict(cnt)
        return d


import ml_dtypes

T = 2048
D = 2048
NT = 16
EPS = 1e-6
N_IN = 6945
GDN_Q0, GDN_K0, GDN_V0, GDN_Z0, GDN_B0, GDN_A0 = 0, 768, 1536, 2304, 3072, 3078
GLA_Q0, GLA_K0, GLA_V0, GLA_G0, GLA_LR0 = 3084, 3404, 3724, 4364, 5004
FOX_Q0, FOX_K0, FOX_V0, FOX_F0 = 5020, 5660, 6300, 6940
DFF = 8192

C_ID, C_MLE, C_MGT, C_NMGT, C_NMLT, C_ONES = 0, 128, 256, 384, 512, 640
C_MGT1 = 768
C_MLE1 = 768 + 129
NCF = 768 + 258
B_ID, B_MNEG, B_SEL = 0, 128, 256
NCB = 256 + 5 * 128
PC_NW1, PC_NW2, PC_CONVG, PC_CONVF, PC_BIASF = 0, 16, 32, 104, 488
NPC = 488 + 128
PR_NPM, PR_NPF, PR_GDNN, PR_GLAN, PR_ALOG, PR_DT, PR_FB = 0, 2048, 4096, 4224, 4352, 4448, 4544
NPR = 4544 + 80


class Rot:
    def __init__(self, items):
        self.items = list(items)
        self.i = 0

    def next(self):
        x = self.items[self.i % len(self.items)]
        self.i += 1
        return x


def make_consts():
    p = np.arange(128)[:, None]
    f = np.arange(128)[None, :]
    cf = np.zeros((128, NCF), np.float32)
    cf[:, C_ID:C_ID + 128] = (p == f)
    cf[:, C_MLE:C_MLE + 128] = (p <= f)
    cf[:, C_MGT:C_MGT + 128] = (p > f)
    cf[:, C_NMGT:C_NMGT + 128] = -1.0 * (p > f)
    cf[:, C_NMLT:C_NMLT + 128] = -1.0 * (p < f)
    cf[:, C_ONES:C_ONES + 128] = 1.0
    cf[:, C_MGT1:C_MGT1 + 128] = (p > f)
    cf[:, C_MGT1 + 128] = 1.0
    cf[:, C_MLE1:C_MLE1 + 128] = (p <= f)
    cf[:, C_MLE1 + 128] = 1.0
    cb = np.zeros((128, NCB), np.float32)
    cb[:, B_ID:B_ID + 128] = (p == f)
    cb[:, B_MNEG:B_MNEG + 128] = -30000.0 * (p > f)
    for h in range(5):
        for r in (h, 32 + h, 64 + h):
            cb[r, B_SEL + h * 128:B_SEL + (h + 1) * 128] = 1.0
    return cf, cb.astype(ml_dtypes.bfloat16)


def build(n_layers=2, debug=False, stop_after=None, parts=("gdn", "gla", "fox")):
    nc = bass.Bass("TRN2", target_bir_lowering=False)
    P = Prog(nc)
    x_d = P.dram("x", [T, D], F32, kind="ExternalInput", nparts=NT)
    w_in_d = P.dram("w_in", [2, D, N_IN], F32, kind="ExternalInput")
    w_out_d = P.dram("w_out", [2, D, D], F32, kind="ExternalInput")
    if stop_after not in ("mix", "norm"):
        w_up_d = P.dram("w_up", [2, D, 2 * DFF], F32, kind="ExternalInput")
        w_down_d = P.dram("w_down", [2, DFF, D], F32, kind="ExternalInput")
    pc_d = P.dram("pcols", [2, 128, NPC], F32, kind="ExternalInput")
    pr_d = P.dram("prows", [2, 128, NPR], F32, kind="ExternalInput")
    wg_d = P.dram("wg", [2, 17, 320], F32, kind="ExternalInput")
    cf_d = P.dram("cf", [128, NCF], F32, kind="ExternalInput")
    cb_d = P.dram("cb", [128, NCB], BF16, kind="ExternalInput")
    y_d = P.dram("y", [T, D], F32, kind="ExternalOutput", nparts=NT)
    dk = "ExternalOutput" if debug else "Internal"
    xa_d = P.dram("xa", [T, D], F32, kind=dk, nparts=NT)
    xb_d = P.dram("xb", [T, D], F32, kind="Internal", nparts=NT)
    oT_d = P.dram("oT", [16, 128, T], BF16, kind=dk, nparts=16)

    PS = [P.psum(f"ps{i}", [128, 512], F32) for i in range(8)]

    cf = P.sb("cf", [128, NCF], F32)
    cb = P.sb("cb", [128, NCB], BF16)
    P.dma("sp", cf[:], cf_d[:], writes=[cf])
    P.dma("sp", cb[:], cb_d[:], writes=[cb])
    ident = cf[:, C_ID:C_ID + 128]
    Mle = cf[:, C_MLE:C_MLE + 128]
    Mgt = cf[:, C_MGT:C_MGT + 128]
    nMgt = cf[:, C_NMGT:C_NMGT + 128]
    nMlt = cf[:, C_NMLT:C_NMLT + 128]
    ones = cf[:, C_ONES:C_ONES + 128]
    Mgt1 = cf[:, C_MGT1:C_MGT1 + 129]
    Mle1 = cf[:, C_MLE1:C_MLE1 + 129]
    identb = cb[:, B_ID:B_ID + 128]
    Mneg = cb[:, B_MNEG:B_MNEG + 128]

    def mm(out, lhsT, rhs, start, stop, reads, writes):
        P.op("pe", lambda e: e.matmul(out, lhsT, rhs, start=start, stop=stop), reads, writes)

    def tr(out, in_, reads, writes):
        P.op("pe", lambda e: e.transpose(out, in_, ident), list(reads) + [cf], writes)

    def act(out, in_, func, reads, writes, bias=None, scale=None, accum=None):
        kw = {}
        if bias is not None:
            kw["bias"] = bias
        if scale is not None:
            kw["scale"] = scale
        if accum is not None:
            kw["accum_out"] = accum
        P.op("act", lambda e: e.activation(out, in_, func, **kw), reads, writes)

    def ts(eng, out, in0, s1, s2, op0, op1, reads, writes):
        if s2 is None:
            P.op(eng, lambda e: e.tensor_scalar(out, in0, s1, None, op0), reads, writes)
        else:
            P.op(eng, lambda e: e.tensor_scalar(out, in0, s1, s2, op0, op1), reads, writes)

    def tt(eng, out, in0, in1, op, reads, writes):
        P.op(eng, lambda e: e.tensor_tensor(out, in0, in1, op), reads, writes)

    def stt(eng, out, in0, scalar, in1, op0, op1, reads, writes):
        P.op(eng, lambda e: e.scalar_tensor_tensor(out, in0, scalar, in1, op0, op1), reads, writes)

    def cp(eng, out, in_, reads, writes):
        if eng == "act":
            P.op("act", lambda e: e.copy(out, in_), reads, writes)
        else:
            P.op(eng, lambda e: e.tensor_copy(out, in_), reads, writes)

    def memset(eng, ap, v, writes):
        P.op(eng, lambda e: e.memset(ap, v), [], writes)

    def load_w(Wd, l, r0, nrt, c0, ncols, buf, bo=0):
        step = 4
        for a in range(0, nrt, step):
            n = min(step, nrt - a)
            src = Wd[l, r0 + a * 128:r0 + (a + n) * 128, c0:c0 + ncols].rearrange("(a p) c -> p a c", p=128)
            P.dma("pool", buf[:, a:a + n, bo:bo + ncols], src, writes=[buf])

    def rstd_cols(ss_ap, out_ap, tmp_ap, n, mean_scale, reads_writes):
        rw = reads_writes
        ts("dve", tmp_ap, ss_ap, mean_scale, EPS, ALU.mult, ALU.add, rw, rw)
        act(tmp_ap, tmp_ap, AF.Sqrt, rw, rw)
        P.op("dve", lambda e: e.reciprocal(out_ap, tmp_ap), rw, rw)

    def norm_phase(src, pcl, col0, hT, tok0, ntt):
        xts = [P.sb("xt", [128, D], F32) for _ in range(2)]
        junk = P.sb("junk", [128, D], F32)
        sts = [P.sb("nst", [128, 4], F32) for _ in range(2)]
        pr = Rot(PS[0:4])
        for i in range(ntt):
            tI = tok0 // 128 + i
            xt = xts[i % 2]
            s = sts[i % 2]
            P.dma("sp", xt[:], src[tI * 128:(tI + 1) * 128, :], reads=[src.part(tI)], writes=[xt])
            import os
            NS_ = int(os.environ.get("NORM_STEPS", "9"))
            memset("dve", s[:, 0:1], 0.0, [s])
            act(junk[:], xt[:], AF.Square, [xt, s], [junk, s], accum=s[:, 0:1])
            if NS_ < 2:
                continue
            rstd_cols(s[:, 0:1], s[:, 2:3], s[:, 1:2], 1, 1.0 / D, [s])
            if NS_ < 3:
                continue
            ts("dve", xt[:], xt[:], s[:, 2:3], None, ALU.mult, None, [xt, s], [xt])
            if NS_ < 4:
                continue
            for g in range(4):
                ps = pr.next()
                for j in range(4):
                    dt = g * 4 + j
                    tr(ps[:, j * 128:(j + 1) * 128], xt[:, dt * 128:(dt + 1) * 128], [xt], [ps])
                if NS_ < 5:
                    continue
                for j in range(4):
                    dt = g * 4 + j
                    o = hT[:, dt, i * 128:(i + 1) * 128]
                    sc = pcl[:, col0 + dt:col0 + dt + 1]
                    if j % 2 == 0:
                        act(o, ps[:, j * 128:(j + 1) * 128], AF.Copy, [ps, pcl], [hT.part(i)], scale=sc)
                    else:
                        ts("dve", o, ps[:, j * 128:(j + 1) * 128], sc, None, ALU.mult, None, [ps, pcl], [hT.part(i)])
        P.release(*xts, junk, *sts)

    def layer(l, src, dst):
        pcl = P.sb("pcl", [128, NPC], F32)
        P.dma("sp", pcl[:], pc_d[l], writes=[pcl])
        hT = P.sb("hT", [128, 16, T], BF16, nparts=NT)
        norm_phase(src, pcl, PC_NW1, hT, 0, NT)
        if stop_after == "norm":
            return
        prl = P.sb("prl", [128, NPR - 4096], F32)
        P.dma("sp", prl[:], pr_d[l, :, 4096:NPR], writes=[prl])
        R0 = 4096
        wbh = {}

        def set_wb(ncols):
            if "rot" in wbh:
                P.release(*wbh["rot"].items)
                del wbh["rot"]
            if ncols:
                wbh["rot"] = Rot([P.sb("wb", [128, 16, ncols], BF16) for _ in range(2)])
        prj = Rot(PS[4:8])

        def fm_proj(c0, M, consumer):
            wt = wbh["rot"].next()
            load_w(w_in_d, l, 0, 16, c0, M, wt)
            for tb in range(4):
                ps = prj.next()
                for dt in range(16):
                    mm(ps[0:M, :], wt[:, dt, 0:M], hT[:, dt, tb * 512:(tb + 1) * 512], dt == 0, dt == 15, [wt, hT], [ps])
                consumer(tb, ps)

        def tm_proj(specs, consumer):
            wt = wbh["rot"].next()
            o = 0
            for (c0, n) in specs:
                load_w(w_in_d, l, 0, 16, c0, n, wt, bo=o)
                o += n
            for tI in range(NT):
                ps = prj.next()
                for dt in range(16):
                    mm(ps[:, 0:o], hT[:, dt, tI * 128:(tI + 1) * 128], wt[:, dt, 0:o], dt == 0, dt == 15, [wt, hT], [ps])
                consumer(tI, ps)

        def finish_head(oc, z_ap, zreads, normrow_ap, oTh, c, st, tmp, psx):
            memset("dve", st[:, 0:1], 0.0, [st])
            act(tmp[:, 0:128], oc, AF.Square, [psx, st], [tmp, st], accum=st[:, 0:1])
            rstd_cols(st[:, 0:1], st[:, 2:3], st[:, 1:2], 1, 1.0 / 128, [st])
            stt("dve", tmp[:, 128:256], oc, st[:, 2:3], normrow_ap, ALU.mult, ALU.mult, [psx, st, prl], [tmp])
            act(tmp[:, 0:128], z_ap, AF.Silu, zreads, [tmp])
            tt("dve", tmp[:, 256:384], tmp[:, 128:256], tmp[:, 0:128], ALU.mult, [tmp], [tmp])

        if "gdn" in parts:
            set_wb(128)
            ba = P.sb("ba", [128, NT, 12], F32)
            gall = P.sb("gall", [128, NT, 6], F32)
            beta = P.sb("beta", [128, NT, 6], F32)
            tm_proj([(GDN_B0, 12)], lambda tI, ps: cp("dve", ba[:, tI, :], ps[:, 0:12], [ps], [ba]))
            act(beta[:], ba[:, :, 0:6], AF.Sigmoid, [ba], [beta])
            for tI in range(NT):
                tt("dve", gall[:, tI, :], ba[:, tI, 6:12], prl[:, PR_DT - R0:PR_DT - R0 + 6], ALU.add, [ba, prl], [gall])
            act(gall[:], gall[:], AF.Softplus, [gall], [gall])
            ea = P.sb("ea", [128, 6], F32)
            act(ea[:], prl[:, PR_ALOG - R0:PR_ALOG - R0 + 6], AF.Exp, [prl], [ea])
            for tI in range(NT):
                stt("dve", gall[:, tI, :], gall[:, tI, :], -1.0, ea[:], ALU.mult, ALU.mult, [gall, ea], [gall])
            for h in range(6):
                zh = P.sb("zh", [128, NT, 128], BF16)
                tm_proj([(GDN_Z0 + h * 128, 128)], lambda tI, ps: cp("act", zh[:, tI, :], ps[:, 0:128], [ps], [zh]))
                raw = P.sb("raw", [128, T], F32)
                qkv = [P.sb("qkvc", [128, T], F32) for _ in range(3)]
                for qi, c0 in enumerate((GDN_Q0, GDN_K0, GDN_V0)):
                    fm_proj(c0 + h * 128, 128, lambda tb, ps: cp("act", raw[:, tb * 512:(tb + 1) * 512], ps[:, :], [ps], [raw]))
                    y = qkv[qi]
                    ct = qi * 6 + h
                    wc = lambda tap: pcl[:, PC_CONVG + ct * 4 + tap:PC_CONVG + ct * 4 + tap + 1]
                    ts("dve", y[:], raw[:], wc(3), None, ALU.mult, None, [raw, pcl], [y])
                    for sft in (1, 2, 3):
                        stt("dve", y[:, sft:], raw[:, 0:T - sft], wc(3 - sft), y[:, sft:], ALU.mult, ALU.add, [raw, pcl, y], [y])
                    act(y[:], y[:], AF.Silu, [y], [y])
                P.release(raw)
                qc, kc, vc = qkv
                u_st = P.sb("u_st", [128, NT, 128], F32, nparts=NT)
                wT_st = P.sb("wT_st", [128, NT, 128], BF16, nparts=NT)
                qgT_st = P.sb("qgT_st", [128, NT, 128], BF16, nparts=NT)
                qkT_st = P.sb("qkT_st", [128, NT, 128], BF16, nparts=NT)
                kend_st = P.sb("kend_st", [128, NT, 128], BF16, nparts=NT)
                egl_st = P.sb("egl_st", [128, NT], F32, nparts=NT)
                oTh = P.sb("oTh", [128, T], BF16)
                NS = 4
                slots = []
                for sI in range(NS):
                    slots.append(dict(
                        tok3=P.sb("tok3", [128, 384], F32), kbqg=P.sb("kbqg", [128, 256], F32),
                        tT=P.sb("tT", [128, 384], F32), rg=P.sb("rg", [128, 258], F32),
                        dm=P.sb("dm", [128, 384], F32), dmm=P.sb("dmm", [128, 384], F32),
                        pp=[P.sb("pp", [128, 256], F32) for _ in range(2)],
                        sol=[P.sb("sol", [128, 256], F32) for _ in range(2)],
                        cs=P.sb("cs", [128, 16], F32), junk=P.sb("jk", [128, 128], F32),
                        X=PS[sI * 2], Y=PS[sI * 2 + 1]))

                def stageA_steps(c, S):
                    tok3, kbqg, tT, rg, dm, dmm, cs, X, Y = S["tok3"], S["kbqg"], S["tT"], S["rg"], S["dm"], S["dmm"], S["cs"], S["X"], S["Y"]
                    sl = slice(c * 128, (c + 1) * 128)
                    tr(X[:, 0:128], qc[:, sl], [qc], [X])
                    tr(X[:, 128:256], kc[:, sl], [kc], [X])
                    tr(X[:, 256:384], vc[:, sl], [vc], [X])
                    memset("dve", cs[:, 0:2], 0.0, [cs])
                    act(S["junk"][:], X[:, 0:128], AF.Square, [X, cs], [S["junk"], cs], accum=cs[:, 0:1])
                    act(S["junk"][:], X[:, 128:256], AF.Square, [X, cs], [S["junk"], cs], accum=cs[:, 1:2])
                    rstd_cols(cs[:, 0:2], cs[:, 4:6], cs[:, 2:4], 2, 1.0, [cs])
                    ts("dve", tok3[:, 0:128], X[:, 0:128], cs[:, 4:5], 128.0 ** -0.5, ALU.mult, ALU.mult, [X, cs], [tok3])
                    act(tok3[:, 128:256], X[:, 128:256], AF.Copy, [X, cs], [tok3], scale=cs[:, 5:6])
                    cp("act", tok3[:, 256:384], X[:, 256:384], [X], [tok3])
                    yield
                    gcol = gall[:, c, h:h + 1]
                    bcol = beta[:, c, h:h + 1]
                    ts("dve", rg[:, 0:129], Mgt1, gcol, None, ALU.mult, None, [gall, cf], [rg])
                    ts("dve", rg[:, 129:258], Mle1, gcol, None, ALU.mult, None, [gall, cf], [rg])
                    mm(Y[:, 0:129], Mle, rg[:, 0:129], True, True, [rg, cf], [Y])
                    mm(Y[:, 256:385], Mgt, rg[:, 129:258], True, True, [rg, cf], [Y])
                    cp("dve", cs[:, 6:7], Y[:, 128:129], [Y], [cs])
                    act(cs[:, 7:8], Y[:, 128:129], AF.Exp, [Y], [cs])
                    act(cs[:, 8:9], Y[:, 384:385], AF.Exp, [Y], [cs])
                    tt("dve", cs[:, 9:10], cs[:, 6:7], Y[:, 384:385], ALU.add, [Y, cs], [cs])
                    act(egl_st[:, c:c + 1], cs[:, 9:10], AF.Exp, [cs], [egl_st.part(c)])
                    act(dm[:, 0:128], Y[:, 0:128], AF.Exp, [Y], [dm])
                    act(dm[:, 128:256], Y[:, 256:384], AF.Exp, [Y], [dm])
                    tt("pool", dmm[:, 0:128], dm[:, 0:128], nMgt, ALU.mult, [dm, cf], [dmm])
                    tt("pool", dmm[:, 128:256], dm[:, 128:256], nMlt, ALU.mult, [dm, cf], [dmm])
                    tt("pool", dmm[:, 256:384], dm[:, 128:256], Mle, ALU.mult, [dm, cf], [dmm])
                    yield
                    ts("dve", kbqg[:, 0:128], tok3[:, 128:256], bcol, None, ALU.mult, None, [tok3, beta], [kbqg])
                    ts("dve", kbqg[:, 128:256], tok3[:, 0:128], cs[:, 7:8], None, ALU.mult, None, [tok3, cs], [kbqg])
                    act(kend_st[:, c, :], tok3[:, 128:256], AF.Copy, [tok3, cs], [kend_st.part(c)], scale=cs[:, 8:9])
                    sol0 = S["sol"][0]
                    ts("dve", sol0[:, 0:128], tok3[:, 256:384], bcol, None, ALU.mult, None, [tok3, beta], [sol0])
                    ts("dve", sol0[:, 128:256], kbqg[:, 0:128], cs[:, 7:8], None, ALU.mult, None, [kbqg, cs], [sol0])
                    tr(X[:, 0:128], tok3[:, 128:256], [tok3], [X])
                    tr(X[:, 128:256], kbqg[:, 0:128], [kbqg], [X])
                    tr(X[:, 256:384], tok3[:, 0:128], [tok3], [X])
                    tr(X[:, 384:512], kbqg[:, 128:256], [kbqg], [X])
                    cp("act", tT[:, :], X[:, 0:384], [X], [tT])
                    cp("dve", qgT_st[:, c, :], X[:, 384:512], [X], [qgT_st.part(c)])
                    yield
                    mm(X[:, 0:128], tT[:, 128:256], tT[:, 0:128], True, True, [tT], [X])
                    mm(X[:, 128:256], tT[:, 0:128], tT[:, 128:256], True, True, [tT], [X])
                    mm(X[:, 256:384], tT[:, 0:128], tT[:, 256:384], True, True, [tT], [X])
                    pp0 = S["pp"][0]
                    tt("dve", pp0[:, 0:256], X[:, 0:256], dmm[:, 0:256], ALU.mult, [X, dmm], [pp0])
                    tt("dve", qkT_st[:, c, :], X[:, 256:384], dmm[:, 256:384], ALU.mult, [X, dmm], [qkT_st.part(c)])
                    yield
                    for k in range(7):
                        ppk = S["pp"][k % 2]
                        ppn = S["pp"][(k + 1) % 2]
                        solk = S["sol"][k % 2]
                        soln = S["sol"][(k + 1) % 2]
                        mm(Y[:, 0:256], ppk[:, 128:256], solk[:, :], True, True, [ppk, solk], [Y])
                        if k < 6:
                            mm(Y[:, 256:384], ppk[:, 128:256], ppk[:, 0:128], True, True, [ppk], [Y])
                            mm(Y[:, 384:512], ppk[:, 0:128], ppk[:, 128:256], True, True, [ppk], [Y])
                        tt("dve", soln[:, :], solk[:, :], Y[:, 0:256], ALU.add, [solk, Y], [soln])
                        if k < 6:
                            cp("act", ppn[:, :], Y[:, 256:512], [Y], [ppn])
                        yield
                    solf = S["sol"][1]
                    cp("pool", u_st[:, c, :], solf[:, 0:128], [solf], [u_st.part(c)])
                    tr(X[:, 0:128], solf[:, 128:256], [solf], [X])
                    cp("act", wT_st[:, c, :], X[:, 0:128], [X], [wT_st.part(c)])
                    yield

                for c0 in range(0, NT, NS):
                    gens = [stageA_steps(c0 + i, slots[i]) for i in range(NS)]
                    alive = True
                    while alive:
                        alive = False
                        for g in gens:
                            try:
                                next(g)
                                alive = True
                            except StopIteration:
                                pass
                for S in slots:
                    P.release(S["tok3"], S["kbqg"], S["tT"], S["rg"], S["dm"], S["dmm"], *S["pp"], *S["sol"], S["cs"], S["junk"])
                P.release(*qkv)
                Sf = P.sb("Sf", [128, 128], F32)
                Sb = P.sb("Sb", [128, 128], BF16)
                vnb = [P.sb("vnb", [128, 128], BF16) for _ in range(2)]
                fst = [P.sb("fst", [128, 4], F32) for _ in range(2)]
                ftmp = [P.sb("ftmp", [128, 384], F32) for _ in range(2)]
                memset("dve", Sf[:], 0.0, [Sf])
                memset("dve", Sb[:], 0.0, [Sb])
                A1, A2, A3, A4 = PS[4], PS[5], PS[6], PS[7]
                for c in range(NT):
                    vn = vnb[c % 2]
                    mm(A1[:, 0:128], wT_st[:, c, :], Sb[:], True, True, [wT_st.part(c), Sb], [A1])
                    tt("dve", vn[:], u_st[:, c, :], A1[:, 0:128], ALU.subtract, [u_st.part(c), A1], [vn])
                    mm(A2[:, 0:128], qgT_st[:, c, :], Sb[:], True, False, [qgT_st.part(c), Sb], [A2])
                    mm(A2[:, 0:128], qkT_st[:, c, :], vn[:], False, True, [qkT_st.part(c), vn], [A2])
                    mm(A3[:, 0:128], kend_st[:, c, :], vn[:], True, True, [kend_st.part(c), vn], [A3])
                    stt("dve", Sf[:], Sf[:], egl_st[:, c:c + 1], A3[:, 0:128], ALU.mult, ALU.add, [Sf, egl_st.part(c), A3], [Sf])
                    cp("act", Sb[:], Sf[:], [Sf], [Sb])
                    st, tmp = fst[c % 2], ftmp[c % 2]
                    finish_head(A2[:, 0:128], zh[:, c, :], [zh], prl[:, PR_GDNN - R0:PR_GDNN - R0 + 128], oTh, c, st, tmp, A2)
                    tr(A4[:, 0:128], tmp[:, 256:384], [tmp], [A4])
                    cp("act", oTh[:, c * 128:(c + 1) * 128], A4[:, 0:128], [A4], [oTh])
                P.dma("act", oT_d[h], oTh[:], reads=[oTh], writes=[oT_d.part(h)])
                P.release(Sf, Sb, *vnb, *fst, *ftmp, u_st, wT_st, qgT_st, qkT_st, kend_st, egl_st, oTh, zh)
            P.release(ba, gall, beta, ea)

        if "gla" in parts:
            set_wb(320)
            lrT = P.sb("lrT", [17, T], F32)
            wga = P.sb("wga", [17, 320], F32)
            P.dma("sp", wga[:], wg_d[l], writes=[wga])
            memset("dve", lrT[:], 1.0, [lrT])
            fm_proj(GLA_LR0, 16, lambda tb, ps: cp("act", lrT[0:16, tb * 512:(tb + 1) * 512], ps[0:16, :], [ps], [lrT]))
            spg = P.sb("spg", [128, NT, 320], F32)
            for tI in range(NT):
                ps = prj.next()
                mm(ps[:, 0:320], lrT[0:17, tI * 128:(tI + 1) * 128], wga[0:17, :], True, True, [lrT, wga], [ps])
                act(spg[:, tI, :], ps[:, 0:320], AF.Softplus, [ps], [spg], scale=-1.0)
            P.release(lrT, wga)
            for h in range(5):
                qT = P.sb("glaqT", [64, T], F32)
                kT = P.sb("glakT", [64, T], F32)
                fm_proj(GLA_Q0 + h * 64, 64, lambda tb, ps: cp("act", qT[:, tb * 512:(tb + 1) * 512], ps[0:64, :], [ps], [qT]))
                fm_proj(GLA_K0 + h * 64, 64, lambda tb, ps: cp("act", kT[:, tb * 512:(tb + 1) * 512], ps[0:64, :], [ps], [kT]))
                kvg = P.sb("kvg", [128, NT, 320], BF16)
                tm_proj([(GLA_K0 + h * 64, 64), (GLA_V0 + h * 128, 128), (GLA_G0 + h * 128, 128)],
                        lambda tI, ps: cp("act", kvg[:, tI, :], ps[:, 0:320], [ps], [kvg]))
                oTh = P.sb("oTh", [128, T], BF16)
                Sf = P.sb("gSf", [64, 128], F32)
                Sb = P.sb("gSb", [64, 128], BF16)
                memset("dve", Sf[:], 0.0, [Sf])
                memset("dve", Sb[:], 0.0, [Sb])
                NB = 2
                eT = [P.sb("eT", [64, 256], F32) for _ in range(NB)]
                qk = [P.sb("qkd", [64, 256], BF16) for _ in range(NB)]
                kiv = [P.sb("kiv", [128, 64], BF16) for _ in range(NB)]
                etok = [P.sb("etok", [128, 64], F32) for _ in range(NB)]
                att = [P.sb("att", [128, 128], BF16) for _ in range(NB)]
                fst = [P.sb("fst", [128, 4], F32) for _ in range(NB)]
                ftmp = [P.sb("ftmp", [128, 384], F32) for _ in range(NB)]
                s1 = [P.sb("gs1", [64, 128], F32) for _ in range(NB)]
                B1, B2, B3, B4 = PS[0], PS[1], PS[2], PS[3]
                for c in range(NT):
                    i2 = c % NB
                    sl = slice(c * 128, (c + 1) * 128)
                    sph = spg[:, c, h * 64:(h + 1) * 64]
                    mm(B1[:, 0:64], Mle, sph, True, True, [cf, spg], [B1])
                    mm(B1[0:64, 128:256], sph, Mle, True, True, [cf, spg], [B1])
                    act(eT[i2][:, 0:128], B1[0:64, 128:256], AF.Exp, [B1], [eT[i2]], scale=-1.0 / 16)
                    act(eT[i2][:, 128:256], B1[0:64, 128:256], AF.Exp, [B1], [eT[i2]], scale=1.0 / 16)
                    act(etok[i2][:], B1[:, 0:64], AF.Exp, [B1], [etok[i2]], scale=1.0 / 16)
                    stt("dve", qk[i2][:, 0:128], qT[:, sl], 0.125, eT[i2][:, 0:128], ALU.mult, ALU.mult, [qT, eT[i2]], [qk[i2]])
                    tt("dve", qk[i2][:, 128:256], kT[:, sl], eT[i2][:, 128:256], ALU.mult, [kT, eT[i2]], [qk[i2]])
                    tt("dve", kiv[i2][:], kvg[:, c, 0:64], etok[i2][:], ALU.mult, [kvg, etok[i2]], [kiv[i2]])
                    mm(B2[:, 0:128], qk[i2][:, 128:256], qk[i2][:, 0:128], True, True, [qk[i2]], [B2])
                    tt("dve", att[i2][:], B2[:, 0:128], Mle, ALU.mult, [B2, cf], [att[i2]])
                    mm(B3[:, 0:128], qk[i2][:, 0:128], Sb[:], True, False, [qk[i2], Sb], [B3])
                    mm(B3[:, 0:128], att[i2][:], kvg[:, c, 64:192], False, True, [att[i2], kvg], [B3])
                    mm(B2[0:64, 128:256], kiv[i2][:], kvg[:, c, 64:192], True, True, [kiv[i2], kvg], [B2])
                    ts("dve", s1[i2][:], Sf[:], eT[i2][:, 127:128], None, ALU.mult, None, [Sf, eT[i2]], [s1[i2]])
                    stt("dve", Sf[:], B2[0:64, 128:256], eT[i2][:, 127:128], s1[i2][:], ALU.mult, ALU.add, [B2, eT[i2], s1[i2]], [Sf])
                    cp("act", Sb[:], Sf[:], [Sf], [Sb])
                    st, tmp = fst[i2], ftmp[i2]
                    finish_head(B3[:, 0:128], kvg[:, c, 192:320], [kvg], prl[:, PR_GLAN - R0:PR_GLAN - R0 + 128], oTh, c, st, tmp, B3)
                    tr(B4[:, 0:128], tmp[:, 256:384], [tmp], [B4])
                    cp("act", oTh[:, sl], B4[:, 0:128], [B4], [oTh])
                P.dma("act", oT_d[6 + h], oTh[:], reads=[oTh], writes=[oT_d.part(6 + h)])
                P.release(qT, kT, kvg, oTh, Sf, Sb, *eT, *qk, *kiv, *etok, *att, *fst, *ftmp, *s1)
            P.release(spg)

        if "fox" in parts:
            set_wb(384)
            vsb = P.sb("vsb", [128, NT, 5, 129], BF16)
            fsb = P.sb("fsb", [128, NT, 5], F32)
            memset("dve", vsb[:], 1.0, [vsb])

            def cons_v1(tI, ps):
                cp("act", vsb[:, tI, 0:4, 0:128], ps[:, 0:512].rearrange("p (h d) -> p h d", h=4), [ps], [vsb])

            def cons_v2(tI, ps):
                cp("act", vsb[:, tI, 4, 0:128], ps[:, 0:128], [ps], [vsb])
                cp("dve", fsb[:, tI, :], ps[:, 128:133], [ps], [fsb])
            tm_proj([(FOX_V0, 384)], lambda tI, ps: cp("act", vsb[:, tI, 0:3, 0:128], ps[:, 0:384].rearrange("p (h d) -> p h d", h=3), [ps], [vsb]))
            tm_proj([(FOX_V0 + 384, 256), (FOX_F0, 5)], lambda tI, ps: (
                cp("act", vsb[:, tI, 3:5, 0:128], ps[:, 0:256].rearrange("p (h d) -> p h d", h=2), [ps], [vsb]),
                cp("dve", fsb[:, tI, :], ps[:, 256:261], [ps], [fsb])))
            for tI in range(NT):
                tt("dve", fsb[:, tI, :], fsb[:, tI, :], prl[:, PR_FB - R0:PR_FB - R0 + 5], ALU.add, [fsb, prl], [fsb])
            act(fsb[:], fsb[:], AF.Softplus, [fsb], [fsb], scale=-1.0)
            cpos = P.sb("cpos", [128, NT, 5], F32)
            offc = P.sb("offc", [128, NT, 5], F32)
            totc = P.sb("totc", [128, NT, 5], F32)
            C1 = PS[0]
            fs2 = fsb[:].rearrange("p t h -> p (t h)")
            mm(C1[:, 0:80], Mle, fs2, True, True, [cf, fsb], [C1])
            mm(C1[:, 128:208], ones, fs2, True, True, [cf, fsb], [C1])
            cp("dve", totc[:].rearrange("p t h -> p (t h)"), C1[:, 128:208], [C1], [totc])
            memset("dve", offc[:, 0, :], 0.0, [offc])
            for tI in range(1, NT):
                tt("dve", offc[:, tI, :], offc[:, tI - 1, :], totc[:, tI - 1, :], ALU.add, [offc, totc], [offc])
            tt("dve", cpos[:].rearrange("p t h -> p (t h)"), C1[:, 0:80], offc[:].rearrange("p t h -> p (t h)"), ALU.add, [C1, offc], [cpos])
            crow = P.sb("crow", [5, T], F32)
            totr = P.sb("totr", [5, NT], F32)
            offr = P.sb("offr", [5, NT], F32)
            for tI in range(NT):
                ps = PS[1 + tI % 2]
                mm(ps[0:5, 0:129], fsb[:, tI, :], Mle1, True, True, [cf, fsb], [ps])
                cp("act", crow[:, tI * 128:(tI + 1) * 128], ps[0:5, 0:128], [ps], [crow])
                cp("dve", totr[:, tI:tI + 1], ps[0:5, 128:129], [ps], [totr])
            memset("dve", offr[:, 0:1], 0.0, [offr])
            for tI in range(1, NT):
                tt("dve", offr[:, tI:tI + 1], offr[:, tI - 1:tI], totr[:, tI - 1:tI], ALU.add, [offr, totr], [offr])
            relc = P.sb("relc", [5, NT], F32)
            for qb in range(4):
                ts("dve", relc[:, qb * 4:(qb + 1) * 4], offr[:, qb * 4:(qb + 1) * 4], offr[:, qb * 4:qb * 4 + 1], None, ALU.subtract, None, [offr], [relc])
            for tI in range(NT):
                ts("dve", crow[:, tI * 128:(tI + 1) * 128], crow[:, tI * 128:(tI + 1) * 128], relc[:, tI:tI + 1], -1.0, ALU.add, ALU.mult, [crow, relc], [crow])
            R3 = P.sb("R3", [69, T], BF16)
            rres = P.sb("rres", [5, T], F32)
            rtmp = P.sb("rtmp", [5, T], F32)
            rb = [P.sb("rb", [5, T], BF16) for _ in range(2)]
            memset("dve", R3[:], 0.0, [R3])
            cp("dve", R3[0:5, :], crow[:, :], [crow], [R3])
            cp("dve", rtmp[:], R3[0:5, :], [R3], [rtmp])
            tt("dve", rres[:], crow[:], rtmp[:], ALU.subtract, [crow, rtmp], [rres])
            cp("dve", rb[0][:], rres[:], [rres], [rb[0]])
            cp("dve", rtmp[:], rb[0][:], [rb[0]], [rtmp])
            tt("dve", rres[:], rres[:], rtmp[:], ALU.subtract, [rres, rtmp], [rres])
            cp("dve", rb[1][:], rres[:], [rres], [rb[1]])
            P.dma("sp", R3[32:37, :], rb[0][:], reads=[rb[0]], writes=[R3])
            P.dma("sp", R3[64:69, :], rb[1][:], reads=[rb[1]], writes=[R3])
            P.release(*rb)
            P.release(crow, totr, offr, relc, rres, rtmp, totc)
            for h in range(5):
                qT = P.sb("fqT", [128, T], BF16)
                kT = P.sb("fkT", [128, T], BF16)
                fm_proj(FOX_Q0 + h * 128, 128, lambda tb, ps: act(qT[:, tb * 512:(tb + 1) * 512], ps[:, :], AF.Copy, [ps], [qT], scale=128.0 ** -0.5))
                fm_proj(FOX_K0 + h * 128, 128, lambda tb, ps: cp("dve", kT[:, tb * 512:(tb + 1) * 512], ps[:, :], [ps], [kT]))
                btab = P.sb("btab", [128, NT, 4], F32)
                for qb in range(4):
                    ts("dve", btab[:, :, qb], cpos[:, :, h], offc[:, 4 * qb, h:h + 1], None, ALU.subtract, None, [cpos, offc], [btab])
                oTh = P.sb("oTh", [128, T], BF16)
                pts = [P.sb("pt", [128, 512], BF16) for _ in range(3)]
                osb = [P.sb("osb", [128, 132], F32) for _ in range(2)]
                sc = Rot(PS[0:2])
                selh = cb[0:69, B_SEL + h * 128:B_SEL + (h + 1) * 128]
                for qb in range(4):
                    accs = [PS[2], PS[3], PS[4], PS[5]]
                    accap = lambda j: accs[j][:, 0:129]
                    for tk in range(4 * qb + 4):
                        j0 = max(0, tk - 4 * qb)
                        q0 = j0 * 128
                        ps = sc.next()
                        diag = tk >= 4 * qb
                        mm(ps[:, q0:512], kT[:, tk * 128:(tk + 1) * 128], qT[:, qb * 512 + q0:(qb + 1) * 512], True, False, [kT, qT], [ps])
                        mm(ps[:, q0:512], selh, R3[0:69, qb * 512 + q0:(qb + 1) * 512], False, not diag, [cb, R3], [ps])
                        if diag:
                            mm(ps[:, q0:q0 + 128], identb, Mneg, False, True, [cb], [ps])
                        pt = pts[tk % 3]
                        act(pt[:, q0:512], ps[:, q0:512], AF.Exp, [ps, btab], [pt], bias=btab[:, tk, qb:qb + 1])
                        for j in range(j0, 4):
                            tq = 4 * qb + j
                            mm(accap(j), pt[:, j * 128:(j + 1) * 128], vsb[:, tk, h, :], tk == 0, tk == tq, [pt, vsb], [accs[j]])
                    for j in range(4):
                        tq = 4 * qb + j
                        ob = osb[j % 2]
                        a = accap(j)
                        P.op("dve", lambda e, ob=ob, a=a: e.reciprocal(ob[:, 128:129], a[:, 128:129]), [accs[j]], [ob])
                        ts("dve", ob[:, 0:128], a[:, 0:128], ob[:, 128:129], None, ALU.mult, None, [accs[j], ob], [ob])
                        pst = PS[6 + j % 2]
                        tr(pst[:, 0:128], ob[:, 0:128], [ob], [pst])
                        cp("act", oTh[:, tq * 128:(tq + 1) * 128], pst[:, 0:128], [pst], [oTh])
                P.dma("act", oT_d[11 + h], oTh[:], reads=[oTh], writes=[oT_d.part(11 + h)])
                P.release(qT, kT, btab, oTh, *pts, *osb)
            P.release(vsb, fsb, cpos, offc, R3)
        set_wb(0)
        P.release(prl, hT)

        oT = P.sb("oTall", [128, 16, T], BF16)
        for ct in range(16):
            P.dma("sp", oT[:, ct, :], oT_d[ct], reads=[oT_d.part(ct)], writes=[oT])
        wo = P.sb("wo", [128, 16, D], BF16)
        for cbk in range(4):
            load_w(w_out_d, l, 0, 16, cbk * 512, 512, wo, bo=cbk * 512)
        nrow = P.sb("nrow", [128, D], F32)
        P.dma("sp", nrow[:], pr_d[l, :, PR_NPM:PR_NPM + D], writes=[nrow])
        xts = [P.sb("xt", [128, D], F32) for _ in range(2)]
        yts = [P.sb("yt", [128, D], F32) for _ in range(2)]
        sts = [P.sb("st", [128, 8], F32) for _ in range(2)]
        junk = P.sb("junk", [128, 512], F32)
        pr = Rot(PS)
        for tI in range(NT):
            xt, yt, st = xts[tI % 2], yts[tI % 2], sts[tI % 2]
            P.dma("sp", xt[:], src[tI * 128:(tI + 1) * 128, :], reads=[src.part(tI)], writes=[xt])
            import os
            OS_ = int(os.environ.get("OP_STEPS", "9"))
            memset("dve", st[:, 0:4], 0.0, [st])
            for cbk in range(4):
                if OS_ < 2:
                    continue
                ps = pr.next()
                for ct in range(16):
                    mm(ps[:, :], oT[:, ct, tI * 128:(tI + 1) * 128], wo[:, ct, cbk * 512:(cbk + 1) * 512], ct == 0, ct == 15, [oT, wo], [ps])
                if OS_ < 3:
                    continue
                act(junk[:], ps[:, :], AF.Square, [ps, st], [junk, st], accum=st[:, cbk:cbk + 1])
                tt("dve", yt[:, cbk * 512:(cbk + 1) * 512], ps[:, :], nrow[:, cbk * 512:(cbk + 1) * 512], ALU.mult, [ps, nrow], [yt])
            if OS_ >= 4:
                P.op("dve", lambda e, st=st: e.tensor_reduce(st[:, 4:5], st[:, 0:4], AX.X, ALU.add), [st], [st])
                rstd_cols(st[:, 4:5], st[:, 6:7], st[:, 5:6], 1, 1.0 / D, [st])
                stt("dve", xt[:], yt[:], st[:, 6:7], xt[:], ALU.mult, ALU.add, [yt, st, xt], [xt])
            P.dma("act", xa_d[tI * 128:(tI + 1) * 128, :], xt[:], reads=[xt], writes=[xa_d.part(tI)])
        P.release(oT, wo, nrow, *xts, *yts, *sts, junk)
        if stop_after == "mix":
            return

        TB = 1024
        NG = 4
        nrow = P.sb("nrow2", [128, D], F32)
        P.dma("sp", nrow[:], pr_d[l, :, PR_NPF:PR_NPF + D], writes=[nrow])
        carry = P.sb("carry", [128, 128, 2], F32)
        for blk in range(T // TB):
            tok0 = blk * TB
            ntt = TB // 128
            h2T = P.sb("h2T", [128, 16, TB], BF16, nparts=ntt)
            norm_phase(xa_d, pcl, PC_NW2, h2T, tok0, ntt)
            ysb = P.sb("ysb", [128, ntt, D], F32, nparts=ntt)
            wup = Rot([P.sb("wup", [128, 16, 256], BF16) for _ in range(3)])
            wdn = Rot([P.sb("wdn", [128, NG, 512], BF16) for _ in range(3)])
            aT = Rot([P.sb("aT", [128, NG, TB], BF16) for _ in range(2)])
            ug = [P.sb("ug", [128, 2, TB + 2], F32) for _ in range(2)]
            cv = [P.sb("cv", [128, 2, TB], F32) for _ in range(2)]
            upr = Rot(PS[0:4])
            dpr = Rot(PS[4:8])
            def up_group(fg):
                    a_t = aT.next()
                    for fi in range(NG):
                        f = fg * NG + fi
                        wt = wup.next()
                        load_w(w_up_d, l, 0, 16, f * 128, 128, wt, bo=0)
                        load_w(w_up_d, l, 0, 16, DFF + f * 128, 128, wt, bo=128)
                        u = ug[f % 2]
                        c = cv[f % 2]
                        for gv in range(2):
                            for tb in range(TB // 512):
                                ps = upr.next()
                                for dt in range(16):
                                    mm(ps[:, :], wt[:, dt, gv * 128:(gv + 1) * 128], h2T[:, dt, tb * 512:(tb + 1) * 512], dt == 0, dt == 15, [wt, h2T], [ps])
                                cp("act", u[:, gv, 2 + tb * 512:2 + (tb + 1) * 512], ps[:, :], [ps], [u])
                            ft = f + 64 * gv
                            if blk == 0:
                                memset("dve", u[:, gv, 0:2], 0.0, [u])
                            else:
                                cp("dve", u[:, gv, 0:2], carry[:, ft, :], [carry], [u])
                        wcf = lambda ft, tap: pcl[:, PC_CONVF + ft * 3 + tap:PC_CONVF + ft * 3 + tap + 1]
                        for gv in range(2):
                            ft = f + 64 * gv
                            eng = "dve"
                            ts(eng, c[:, gv, :], u[:, gv, 2:TB + 2], wcf(ft, 2), pcl[:, PC_BIASF + ft:PC_BIASF + ft + 1], ALU.mult, ALU.add, [u, pcl], [c])
                            stt(eng, c[:, gv, :], u[:, gv, 1:TB + 1], wcf(ft, 1), c[:, gv, :], ALU.mult, ALU.add, [u, pcl, c], [c])
                            stt(eng, c[:, gv, :], u[:, gv, 0:TB], wcf(ft, 0), c[:, gv, :], ALU.mult, ALU.add, [u, pcl, c], [c])
                            if blk < T // TB - 1:
                                cp("dve", carry[:, ft, :], u[:, gv, TB:TB + 2], [u], [carry])
                        act(c[:, 0, :], c[:, 0, :], AF.Gelu_apprx_tanh, [c], [c])
                        tt("dve", a_t[:, fi, :], c[:, 0, :], c[:, 1, :], ALU.mult, [c], [a_t])
                    return a_t

            def down_group(fg, a_t):
                    for cbk in range(4):
                        wd = wdn.next()
                        load_w(w_down_d, l, fg * NG * 128, NG, cbk * 512, 512, wd)
                        for tI in range(ntt):
                            ps = dpr.next()
                            for fi in range(NG):
                                mm(ps[:, :], a_t[:, fi, tI * 128:(tI + 1) * 128], wd[:, fi, :], fi == 0, fi == NG - 1, [a_t, wd], [ps])
                            o = ysb[:, tI, cbk * 512:(cbk + 1) * 512]
                            if fg == 0:
                                cp("act", o, ps[:, :], [ps], [ysb.part(tI)])
                            else:
                                tt("dve", o, o, ps[:, :], ALU.add, [ps, ysb.part(tI)], [ysb.part(tI)])

            nfg = 64 // NG
            a_prev = up_group(0)
            for fg in range(1, nfg):
                a_cur = up_group(fg)
                down_group(fg - 1, a_prev)
                a_prev = a_cur
            down_group(nfg - 1, a_prev)
            P.release(*wup.items, *wdn.items, *aT.items, *ug, *cv, h2T)
            xts = [P.sb("xt", [128, D], F32) for _ in range(2)]
            sts = [P.sb("st", [128, 4], F32) for _ in range(2)]
            junk = P.sb("junk", [128, D], F32)
            for i in range(ntt):
                tI = tok0 // 128 + i
                xt, st = xts[i % 2], sts[i % 2]
                P.dma("sp", xt[:], xa_d[tI * 128:(tI + 1) * 128, :], reads=[xa_d.part(tI)], writes=[xt])
                memset("dve", st[:, 0:1], 0.0, [st])
                act(junk[:], ysb[:, i, :], AF.Square, [ysb.part(i), st], [junk, st], accum=st[:, 0:1])
                rstd_cols(st[:, 0:1], st[:, 2:3], st[:, 1:2], 1, 1.0 / D, [st])
                tt("pool", ysb[:, i, :], ysb[:, i, :], nrow[:], ALU.mult, [ysb.part(i), nrow], [ysb.part(i)])
                stt("dve", xt[:], ysb[:, i, :], st[:, 2:3], xt[:], ALU.mult, ALU.add, [ysb.part(i), st, xt], [xt])
                P.dma("act", dst[tI * 128:(tI + 1) * 128, :], xt[:], reads=[xt], writes=[dst.part(tI)])
            P.release(*xts, *sts, junk, ysb)
            if blk == T // TB - 1:
                P.release(carry)
        P.release(nrow, pcl)

    for l in range(n_layers):
        src = x_d if l == 0 else xb_d
        dst = xb_d if l < n_layers - 1 else y_d
        layer(l, src, dst)
        if stop_after is not None:
            break
    counts = P.emit()
    return nc, counts


def prep_params(inp):
    f32 = np.float32
    pc = np.zeros((2, 128, NPC), f32)
    pr = np.zeros((2, 128, NPR), f32)
    wg = np.zeros((2, 17, 320), f32)
    for l in range(2):
        pc[l, :, PC_NW1:PC_NW1 + 16] = inp["norm_pre_mix"][l].reshape(16, 128).T
        pc[l, :, PC_NW2:PC_NW2 + 16] = inp["norm_pre_ffn"][l].reshape(16, 128).T
        cg = inp["conv_gdn"][l]
        pc[l, :, PC_CONVG:PC_CONVG + 72] = cg.reshape(4, 18, 128).transpose(2, 1, 0).reshape(128, 72)
        cfw = inp["conv_ffn"][l]
        pc[l, :, PC_CONVF:PC_CONVF + 384] = cfw.reshape(3, 128, 128).transpose(2, 1, 0).reshape(128, 384)
        pc[l, :, PC_BIASF:PC_BIASF + 128] = inp["conv_ffn_bias"][l].reshape(128, 128).T
        pr[l, :, PR_NPM:PR_NPM + 2048] = inp["norm_post_mix"][l][None, :]
        pr[l, :, PR_NPF:PR_NPF + 2048] = inp["norm_post_ffn"][l][None, :]
        pr[l, :, PR_GDNN:PR_GDNN + 128] = inp["gdn_norm"][l][None, :]
        pr[l, :, PR_GLAN:PR_GLAN + 128] = inp["gla_norm"][l][None, :]
        pr[l, :, PR_ALOG:PR_ALOG + 6] = inp["gdn_a_log"][l][None, :]
        pr[l, :, PR_DT:PR_DT + 6] = inp["gdn_dt_bias"][l][None, :]
        pr[l, :, PR_FB:PR_FB + 5] = inp["fox_f_bias"][l][None, :]
        wg[l, 0:16] = inp["gla_w_gate"][l]
        wg[l, 16] = inp["gla_b_gate"][l]
    return pc, pr, wg


_CACHE = {}


def kernel(**inputs):
    inp = {k: np.asarray(v) for k, v in inputs.items()}
    if "nc" not in _CACHE:
        _CACHE["nc"] = build(2)[0]
    nc = _CACHE["nc"]
    pc, pr, wg = prep_params(inp)
    cf, cb = make_consts()
    x = np.ascontiguousarray(inp["x"], dtype=np.float32)
    shared = {"w_in": np.ascontiguousarray(inp["w_in"], dtype=np.float32),
              "w_out": np.ascontiguousarray(inp["w_out"], dtype=np.float32),
              "w_up": np.ascontiguousarray(inp["w_up"], dtype=np.float32),
              "w_down": np.ascontiguousarray(inp["w_down"], dtype=np.float32),
              "pcols": pc, "prows": pr, "wg": wg, "cf": cf, "cb": cb}
    in_maps = [dict(shared, x=x[b]) for b in range(8)]
    res = run_bass_kernel_spmd(nc, in_maps, core_ids=list(range(8)))
    return np.stack([np.asarray(r["y"], dtype=np.float32) for r in res.results], axis=0)
```

```python
import numpy as np
import concourse.bass as bass
import concourse.mybir as mybir
from concourse.bass_utils import run_bass_kernel_spmd

F32 = mybir.dt.float32
BF16 = mybir.dt.bfloat16
ALU = mybir.AluOpType
AF = mybir.ActivationFunctionType
AX = mybir.AxisListType

SBUF_BASE = 16512 + 2048
SBUF_END = 229376


class Dep:
    __slots__ = ("lw", "rd", "name", "excl", "cow", "cow_wait", "cow_ws")

    def __init__(self, name=""):
        self.lw = None
        self.rd = []
        self.name = name
        self.excl = False
        self.cow = False
        self.cow_wait = []
        self.cow_ws = []


class Buf:
    def __init__(self, name, handle, nparts=1, off=None, size=None):
        self.name = name
        self.h = handle
        self.parts = [Dep(f"{name}.{i}") for i in range(nparts)]
        self.off = off
        self.size = size

    def __getitem__(self, key):
        return self.h[key]

    def part(self, i):
        return self.parts[i]

    @property
    def all(self):
        return self.parts


class Op:
    __slots__ = ("eng", "fn", "waits", "is_dma", "slot", "seq", "signals", "tick", "idx")


class Prog:
    ENGS = ("pe", "act", "dve", "pool", "sp")
    NSLOT = {"sp": 8, "pool": 8, "act": 6}

    def __init__(self, nc):
        self.nc = nc
        self.ops = []
        self.dma_count = {q: 0 for q in self.NSLOT}
        self.free = [(SBUF_BASE, SBUF_END - SBUF_BASE)]
        self.retired = []
        self.nbuf = 0
        self.psum_banks = None

    def sb(self, name, shape, dtype, nparts=1):
        esz = 4 if dtype == F32 else 2
        n = 1
        for s in shape[1:]:
            n *= s
        size = (n * esz + 63) // 64 * 64
        for i, (o, s) in enumerate(self.free):
            if s >= size:
                off = o
                if s == size:
                    self.free.pop(i)
                else:
                    self.free[i] = (o + size, s - size)
                break
        else:
            raise RuntimeError(f"SBUF OOM allocating {name} {shape} size {size}; free={self.free}")
        self.nbuf += 1
        h = self.nc.alloc_sbuf_tensor_at(f"{name}_{self.nbuf}", list(shape), dtype, offset=off)
        b = Buf(name, h, nparts, off, size)
        keep = []
        for (ro, rs, deps) in self.retired:
            if ro < off + size and off < ro + rs:
                for d in deps:
                    for p in b.parts:
                        if d.lw is not None:
                            p.rd.append(d.lw)
                        p.rd.extend(d.rd)
                if not (off <= ro and ro + rs <= off + size):
                    keep.append((ro, rs, deps))
            else:
                keep.append((ro, rs, deps))
        self.retired = keep
        return b

    def release(self, *bufs):
        for b in bufs:
            self.retired.append((b.off, b.size, list(b.parts)))
            self.free.append((b.off, b.size))
        self.free.sort()
        m = []
        for o, s in self.free:
            if m and m[-1][0] + m[-1][1] == o:
                m[-1] = (m[-1][0], m[-1][1] + s)
            else:
                m.append((o, s))
        self.free = m

    def dram(self, name, shape, dtype, kind="Internal", nparts=1):
        h = self.nc.dram_tensor(name, list(shape), dtype, kind=kind)
        return Buf(name, h, nparts)

    def psum(self, name, shape, dtype=F32, nparts=1):
        h = self.nc.alloc_psum_tensor(name, list(shape), dtype)
        b = Buf(name, h, nparts)
        for p in b.parts:
            p.excl = True
        return b

    def _deps(self, reads, writes):
        raw, war = set(), set()
        for d in reads:
            if d.lw is not None:
                raw.add(d.lw)
            for w in d.cow_ws:
                raw.add(w)
        for d in writes:
            if d.lw is not None:
                war.add(d.lw)
            for w in d.cow_ws:
                war.add(w)
            for r in d.rd:
                war.add(r)
        return raw, war

    def _flat(self, lst):
        out = []
        for x in lst:
            if isinstance(x, Buf):
                out.extend(x.parts)
            elif isinstance(x, Dep):
                out.append(x)
            else:
                out.extend(self._flat(x))
        return out

    def op(self, eng, fn, reads=(), writes=()):
        reads = self._flat(reads)
        writes = self._flat(writes)
        ex = [d for d in reads if d.excl]
        if ex:
            reads = [d for d in reads if not d.excl]
            writes = writes + [d for d in ex if d not in writes]
        o = Op()
        o.eng = eng
        o.fn = fn
        o.is_dma = False
        o.signals = False
        o.tick = None
        o.idx = len(self.ops)
        raw, war = self._deps(reads, writes)
        waits = []
        for p in raw:
            if p.is_dma or p.eng != eng or eng != "pe":
                waits.append(p)
        for p in war:
            if p in raw:
                continue
            if p.is_dma or p.eng != eng:
                waits.append(p)
        o.waits = waits
        for p in waits:
            if not p.is_dma:
                p.signals = True
        for d in reads:
            d.rd.append(o)
            d.cow = False
        for d in writes:
            d.lw = o
            d.rd = []
            d.cow = False
            d.cow_ws = []
        self.ops.append(o)
        return o

    def dma(self, q, out, in_, reads=(), writes=(), cowrite=False):
        reads = self._flat(reads)
        writes = self._flat(writes)
        o = Op()
        o.eng = q
        o.fn = (out, in_)
        o.is_dma = True
        o.signals = False
        o.tick = None
        o.idx = len(self.ops)
        n = self.dma_count[q]
        self.dma_count[q] += 1
        o.slot = n % self.NSLOT[q]
        o.seq = n // self.NSLOT[q]
        join = cowrite and len(writes) > 0 and all(d.cow for d in writes)
        if join:
            raw, _ = self._deps(reads, [])
            ws = set(raw)
            for d in writes:
                ws.update(d.cow_wait)
            waits = list(ws)
        else:
            raw, war = self._deps(reads, writes)
            waits = list(raw | war)
        o.waits = waits
        for p in waits:
            if not p.is_dma:
                p.signals = True
        for d in reads:
            d.rd.append(o)
            d.cow = False
        for d in writes:
            if join:
                d.cow_ws.append(o)
            else:
                d.cow_wait = list(waits)
                d.lw = o
                d.rd = []
                d.cow_ws = []
                d.cow = cowrite
        self.ops.append(o)
        return o

    CLIM = 4000
    DLIM = 250

    def emit(self):
        nc = self.nc
        import contextlib
        es = contextlib.ExitStack()
        with es:
            semtab = {}

            def getsem(key):
                if key not in semtab:
                    semtab[key] = es.enter_context(nc.semaphore("s_" + "_".join(str(k) for k in key)))
                return semtab[key]

            cnt = {e: 0 for e in ("pe", "act", "dve", "pool")}
            for o in self.ops:
                if not o.is_dma and o.signals:
                    c = cnt[o.eng]
                    cnt[o.eng] += 1
                    o.tick = (c // self.CLIM, c % self.CLIM + 1)
            streams = {e: [] for e in self.ENGS}
            seen = {e: {} for e in self.ENGS}

            def sig(p):
                if p.is_dma:
                    ep, sq = p.seq // self.DLIM, p.seq % self.DLIM
                    return ("d", p.eng, p.slot), (ep, 16 * (sq + 1)), getsem(("d", p.eng, p.slot, ep))
                return ("c", p.eng), p.tick, getsem(("c", p.eng, p.tick[0]))

            last_dma = {}
            for o in self.ops:
                st = streams[o.eng]
                sn = seen[o.eng]
                best = {}
                plist = list(o.waits)
                if o.is_dma:
                    prev = last_dma.get((o.eng, o.slot))
                    if prev is not None:
                        plist.append(prev)
                    last_dma[(o.eng, o.slot)] = o
                for p in plist:
                    k, v, s = sig(p)
                    if sn.get(k, (-1, 0)) >= v:
                        continue
                    if k not in best or best[k][0] < v:
                        best[k] = (v, s)
                for k, (v, s) in best.items():
                    sn[k] = v
                    st.append(("w", s, v[1]))
                if o.is_dma:
                    ep = o.seq // self.DLIM
                    st.append(("d", o.fn, getsem(("d", o.eng, o.slot, ep))))
                else:
                    st.append(("c", o.fn, getsem(("c", o.eng, o.tick[0])) if o.signals else None))
            st = streams["sp"]
            for (q, sl), p in last_dma.items():
                k, v, s = sig(p)
                if seen["sp"].get(k, (-1, 0)) < v:
                    st.append(("w", s, v[1]))
            self.nsems = len(semtab)

            def run(eng_obj, items):
                for it in items:
                    if it[0] == "w":
                        eng_obj.wait_ge(it[1], it[2])
                    elif it[0] == "d":
                        out, in_ = it[1]
                        eng_obj.dma_start(out=out, in_=in_).then_inc(it[2], 16)
                    else:
                        ins = it[1](eng_obj)
                        if it[2] is not None:
                            ins.then_inc(it[2], 1)

            with nc.Block() as block:
                @block.tensor
                def _(e):
                    run(e, streams["pe"])

                @block.scalar
                def _(e):
                    run(e, streams["act"])

                @block.vector
                def _(e):
                    run(e, streams["dve"])

                @block.gpsimd
                def _(e):
                    run(e, streams["pool"])

                @block.sync
                def _(e):
                    run(e, streams["sp"])
        d = {e: len(s) for e, s in streams.items()}
        d["sems"] = self.nsems
        d["ticks"] = dict(cnt)
        return d


import ml_dtypes

T = 2048
D = 2048
NT = 16
EPS = 1e-6
N_IN = 6945
GDN_Q0, GDN_K0, GDN_V0, GDN_Z0, GDN_B0, GDN_A0 = 0, 768, 1536, 2304, 3072, 3078
GLA_Q0, GLA_K0, GLA_V0, GLA_G0, GLA_LR0 = 3084, 3404, 3724, 4364, 5004
FOX_Q0, FOX_K0, FOX_V0, FOX_F0 = 5020, 5660, 6300, 6940
DFF = 8192

C_ID, C_MLE, C_MGT, C_NMGT, C_NMLT, C_ONES = 0, 128, 256, 384, 512, 640
C_MGT1 = 768
C_MLE1 = 768 + 129
NCF = 768 + 258
B_ID, B_MNEG, B_SEL = 0, 128, 256
NCB = 256 + 5 * 128
PC_NW1, PC_NW2, PC_CONVG, PC_CONVF, PC_BIASF = 0, 16, 32, 104, 488
NPC = 488 + 128
PR_NPM, PR_NPF, PR_GDNN, PR_GLAN, PR_ALOG, PR_DT, PR_FB = 0, 2048, 4096, 4224, 4352, 4448, 4544
NPR = 4544 + 80


class Rot:
    def __init__(self, items):
        self.items = list(items)
        self.i = 0

    def next(self):
        x = self.items[self.i % len(self.items)]
        self.i += 1
        return x


def make_consts():
    p = np.arange(128)[:, None]
    f = np.arange(128)[None, :]
    cf = np.zeros((128, NCF), np.float32)
    cf[:, C_ID:C_ID + 128] = (p == f)
    cf[:, C_MLE:C_MLE + 128] = (p <= f)
    cf[:, C_MGT:C_MGT + 128] = (p > f)
    cf[:, C_NMGT:C_NMGT + 128] = -1.0 * (p > f)
    cf[:, C_NMLT:C_NMLT + 128] = -1.0 * (p < f)
    cf[:, C_ONES:C_ONES + 128] = 1.0
    cf[:, C_MGT1:C_MGT1 + 128] = (p > f)
    cf[:, C_MGT1 + 128] = 1.0
    cf[:, C_MLE1:C_MLE1 + 128] = (p <= f)
    cf[:, C_MLE1 + 128] = 1.0
    cb = np.zeros((128, NCB), np.float32)
    cb[:, B_ID:B_ID + 128] = (p == f)
    cb[:, B_MNEG:B_MNEG + 128] = -30000.0 * (p > f)
    for h in range(5):
        for r in (h, 32 + h, 64 + h):
            cb[r, B_SEL + h * 128:B_SEL + (h + 1) * 128] = 1.0
    return cf, cb.astype(ml_dtypes.bfloat16)


def build(n_layers=2, debug=False, stop_after=None, parts=("gdn", "gla", "fox")):
    nc = bass.Bass("TRN2", target_bir_lowering=False)
    P = Prog(nc)
    x_d = P.dram("x", [T, D], F32, kind="ExternalInput", nparts=NT)
    w_in_d = P.dram("w_in", [2, D, N_IN], F32, kind="ExternalInput")
    w_out_d = P.dram("w_out", [2, D, D], F32, kind="ExternalInput")
    if stop_after not in ("mix", "norm"):
        w_up_d = P.dram("w_up", [2, D, 2 * DFF], F32, kind="ExternalInput")
        w_down_d = P.dram("w_down", [2, DFF, D], F32, kind="ExternalInput")
    pc_d = P.dram("pcols", [2, 128, NPC], F32, kind="ExternalInput")
    pr_d = P.dram("prows", [2, 128, NPR], F32, kind="ExternalInput")
    wg_d = P.dram("wg", [2, 17, 320], F32, kind="ExternalInput")
    cf_d = P.dram("cf", [128, NCF], F32, kind="ExternalInput")
    cb_d = P.dram("cb", [128, NCB], BF16, kind="ExternalInput")
    y_d = P.dram("y", [T, D], F32, kind="ExternalOutput", nparts=NT)
    dk = "ExternalOutput" if debug else "Internal"
    xa_d = P.dram("xa", [T, D], F32, kind=dk, nparts=NT)
    xb_d = P.dram("xb", [T, D], F32, kind="Internal", nparts=NT)
    oT_d = P.dram("oT", [16, 128, T], BF16, kind=dk, nparts=16)

    PS = [P.psum(f"ps{i}", [128, 512], F32) for i in range(8)]

    cf = P.sb("cf", [128, NCF], F32)
    cb = P.sb("cb", [128, NCB], BF16)
    P.dma("sp", cf[:], cf_d[:], writes=[cf])
    P.dma("sp", cb[:], cb_d[:], writes=[cb])
    ident = cf[:, C_ID:C_ID + 128]
    Mle = cf[:, C_MLE:C_MLE + 128]
    Mgt = cf[:, C_MGT:C_MGT + 128]
    nMgt = cf[:, C_NMGT:C_NMGT + 128]
    nMlt = cf[:, C_NMLT:C_NMLT + 128]
    ones = cf[:, C_ONES:C_ONES + 128]
    Mgt1 = cf[:, C_MGT1:C_MGT1 + 129]
    Mle1 = cf[:, C_MLE1:C_MLE1 + 129]
    identb = cb[:, B_ID:B_ID + 128]
    Mneg = cb[:, B_MNEG:B_MNEG + 128]

    def mm(out, lhsT, rhs, start, stop, reads, writes):
        P.op("pe", lambda e: e.matmul(out, lhsT, rhs, start=start, stop=stop), reads, writes)

    def tr(out, in_, reads, writes):
        P.op("pe", lambda e: e.transpose(out, in_, ident), list(reads) + [cf], writes)

    def act(out, in_, func, reads, writes, bias=None, scale=None, accum=None):
        kw = {}
        if bias is not None:
            kw["bias"] = bias
        if scale is not None:
            kw["scale"] = scale
        if accum is not None:
            kw["accum_out"] = accum
        P.op("act", lambda e: e.activation(out, in_, func, **kw), reads, writes)

    def ts(eng, out, in0, s1, s2, op0, op1, reads, writes):
        if s2 is None:
            P.op(eng, lambda e: e.tensor_scalar(out, in0, s1, None, op0), reads, writes)
        else:
            P.op(eng, lambda e: e.tensor_scalar(out, in0, s1, s2, op0, op1), reads, writes)

    def tt(eng, out, in0, in1, op, reads, writes):
        P.op(eng, lambda e: e.tensor_tensor(out, in0, in1, op), reads, writes)

    def stt(eng, out, in0, scalar, in1, op0, op1, reads, writes):
        P.op(eng, lambda e: e.scalar_tensor_tensor(out, in0, scalar, in1, op0, op1), reads, writes)

    def cp(eng, out, in_, reads, writes):
        if eng == "act":
            P.op("act", lambda e: e.copy(out, in_), reads, writes)
        else:
            P.op(eng, lambda e: e.tensor_copy(out, in_), reads, writes)

    def memset(eng, ap, v, writes):
        P.op(eng, lambda e: e.memset(ap, v), [], writes)

    def load_w(Wd, l, r0, nrt, c0, ncols, buf, bo=0):
        step = 4
        for a in range(0, nrt, step):
            n = min(step, nrt - a)
            src = Wd[l, r0 + a * 128:r0 + (a + n) * 128, c0:c0 + ncols].rearrange("(a p) c -> p a c", p=128)
            P.dma("pool", buf[:, a:a + n, bo:bo + ncols], src, writes=[buf], cowrite=True)

    def rstd_cols(ss_ap, out_ap, tmp_ap, n, mean_scale, reads_writes):
        rw = reads_writes
        ts("dve", tmp_ap, ss_ap, mean_scale, EPS, ALU.mult, ALU.add, rw, rw)
        act(tmp_ap, tmp_ap, AF.Sqrt, rw, rw)
        P.op("dve", lambda e: e.reciprocal(out_ap, tmp_ap), rw, rw)

    def norm_phase(src, pcl, col0, hT, tok0, ntt):
        xts = [P.sb("xt", [128, D], F32) for _ in range(2)]
        junk = P.sb("junk", [128, D], F32)
        sts = [P.sb("nst", [128, 4], F32) for _ in range(2)]
        pr = Rot(PS[0:4])
        for i in range(ntt):
            tI = tok0 // 128 + i
            xt = xts[i % 2]
            s = sts[i % 2]
            P.dma("sp", xt[:], src[tI * 128:(tI + 1) * 128, :], reads=[src.part(tI)], writes=[xt])
            import os
            NS_ = int(os.environ.get("NORM_STEPS", "9"))
            memset("dve", s[:, 0:1], 0.0, [s])
            act(junk[:], xt[:], AF.Square, [xt, s], [junk, s], accum=s[:, 0:1])
            if NS_ < 2:
                continue
            rstd_cols(s[:, 0:1], s[:, 2:3], s[:, 1:2], 1, 1.0 / D, [s])
            if NS_ < 3:
                continue
            ts("dve", xt[:], xt[:], s[:, 2:3], None, ALU.mult, None, [xt, s], [xt])
            if NS_ < 4:
                continue
            for g in range(4):
                ps = pr.next()
                for j in range(4):
                    dt = g * 4 + j
                    tr(ps[:, j * 128:(j + 1) * 128], xt[:, dt * 128:(dt + 1) * 128], [xt], [ps])
                if NS_ < 5:
                    continue
                for j in range(4):
                    dt = g * 4 + j
                    o = hT[:, dt, i * 128:(i + 1) * 128]
                    sc = pcl[:, col0 + dt:col0 + dt + 1]
                    if j % 2 == 0:
                        act(o, ps[:, j * 128:(j + 1) * 128], AF.Copy, [ps, pcl], [hT.part(i)], scale=sc)
                    else:
                        ts("dve", o, ps[:, j * 128:(j + 1) * 128], sc, None, ALU.mult, None, [ps, pcl], [hT.part(i)])
        P.release(*xts, junk, *sts)

    def layer(l, src, dst):
        pcl = P.sb("pcl", [128, NPC], F32)
        P.dma("sp", pcl[:], pc_d[l], writes=[pcl])
        hT = P.sb("hT", [128, 16, T], BF16, nparts=NT)
        norm_phase(src, pcl, PC_NW1, hT, 0, NT)
        if stop_after == "norm":
            return
        prl = P.sb("prl", [128, NPR - 4096], F32)
        P.dma("sp", prl[:], pr_d[l, :, 4096:NPR], writes=[prl])
        R0 = 4096
        wbh = {}

        def set_wb(ncols):
            if "rot" in wbh:
                P.release(*wbh["rot"].items)
                del wbh["rot"]
            if ncols:
                wbh["rot"] = Rot([P.sb("wb", [128, 16, ncols], BF16) for _ in range(2)])
        prj = Rot(PS[4:8])

        def fm_proj(c0, M, consumer):
            wt = wbh["rot"].next()
            load_w(w_in_d, l, 0, 16, c0, M, wt)
            for tb in range(4):
                ps = prj.next()
                for dt in range(16):
                    mm(ps[0:M, :], wt[:, dt, 0:M], hT[:, dt, tb * 512:(tb + 1) * 512], dt == 0, dt == 15, [wt, hT], [ps])
                consumer(tb, ps)

        def tm_proj(specs, consumer):
            wt = wbh["rot"].next()
            o = 0
            for (c0, n) in specs:
                load_w(w_in_d, l, 0, 16, c0, n, wt, bo=o)
                o += n
            for tI in range(NT):
                ps = prj.next()
                for dt in range(16):
                    mm(ps[:, 0:o], hT[:, dt, tI * 128:(tI + 1) * 128], wt[:, dt, 0:o], dt == 0, dt == 15, [wt, hT], [ps])
                consumer(tI, ps)

        def finish_head(oc, z_ap, zreads, normrow_ap, oTh, c, st, tmp, psx):
            memset("dve", st[:, 0:1], 0.0, [st])
            act(tmp[:, 0:128], oc, AF.Square, [psx, st], [tmp, st], accum=st[:, 0:1])
            rstd_cols(st[:, 0:1], st[:, 2:3], st[:, 1:2], 1, 1.0 / 128, [st])
            stt("dve", tmp[:, 128:256], oc, st[:, 2:3], normrow_ap, ALU.mult, ALU.mult, [psx, st, prl], [tmp])
            act(tmp[:, 0:128], z_ap, AF.Silu, zreads, [tmp])
            tt("dve", tmp[:, 256:384], tmp[:, 128:256], tmp[:, 0:128], ALU.mult, [tmp], [tmp])

        if "gdn" in parts:
            set_wb(128)
            ba = P.sb("ba", [128, NT, 12], F32)
            gall = P.sb("gall", [128, NT, 6], F32)
            beta = P.sb("beta", [128, NT, 6], F32)
            tm_proj([(GDN_B0, 12)], lambda tI, ps: cp("dve", ba[:, tI, :], ps[:, 0:12], [ps], [ba]))
            act(beta[:], ba[:, :, 0:6], AF.Sigmoid, [ba], [beta])
            for tI in range(NT):
                tt("dve", gall[:, tI, :], ba[:, tI, 6:12], prl[:, PR_DT - R0:PR_DT - R0 + 6], ALU.add, [ba, prl], [gall])
            act(gall[:], gall[:], AF.Softplus, [gall], [gall])
            ea = P.sb("ea", [128, 6], F32)
            act(ea[:], prl[:, PR_ALOG - R0:PR_ALOG - R0 + 6], AF.Exp, [prl], [ea])
            for tI in range(NT):
                stt("dve", gall[:, tI, :], gall[:, tI, :], -1.0, ea[:], ALU.mult, ALU.mult, [gall, ea], [gall])
            for h in range(6):
                zh = P.sb("zh", [128, NT, 128], BF16)
                tm_proj([(GDN_Z0 + h * 128, 128)], lambda tI, ps: cp("act", zh[:, tI, :], ps[:, 0:128], [ps], [zh]))
                raw = P.sb("raw", [128, T], F32)
                qkv = [P.sb("qkvc", [128, T], F32) for _ in range(3)]
                for qi, c0 in enumerate((GDN_Q0, GDN_K0, GDN_V0)):
                    fm_proj(c0 + h * 128, 128, lambda tb, ps: cp("act", raw[:, tb * 512:(tb + 1) * 512], ps[:, :], [ps], [raw]))
                    y = qkv[qi]
                    ct = qi * 6 + h
                    wc = lambda tap: pcl[:, PC_CONVG + ct * 4 + tap:PC_CONVG + ct * 4 + tap + 1]
                    ts("dve", y[:], raw[:], wc(3), None, ALU.mult, None, [raw, pcl], [y])
                    for sft in (1, 2, 3):
                        stt("dve", y[:, sft:], raw[:, 0:T - sft], wc(3 - sft), y[:, sft:], ALU.mult, ALU.add, [raw, pcl, y], [y])
                    act(y[:], y[:], AF.Silu, [y], [y])
                P.release(raw)
                qc, kc, vc = qkv
                u_st = P.sb("u_st", [128, NT, 128], F32, nparts=NT)
                wT_st = P.sb("wT_st", [128, NT, 128], BF16, nparts=NT)
                qgT_st = P.sb("qgT_st", [128, NT, 128], BF16, nparts=NT)
                qkT_st = P.sb("qkT_st", [128, NT, 128], BF16, nparts=NT)
                kend_st = P.sb("kend_st", [128, NT, 128], BF16, nparts=NT)
                egl_st = P.sb("egl_st", [128, NT], F32, nparts=NT)
                oTh = P.sb("oTh", [128, T], BF16)
                NS = 4
                slots = []
                for sI in range(NS):
                    slots.append(dict(
                        tok3=P.sb("tok3", [128, 384], F32), kbqg=P.sb("kbqg", [128, 256], F32),
                        tT=P.sb("tT", [128, 384], F32), rg=P.sb("rg", [128, 258], F32),
                        dm=P.sb("dm", [128, 384], F32), dmm=P.sb("dmm", [128, 384], F32),
                        pp=[P.sb("pp", [128, 256], F32) for _ in range(2)],
                        sol=[P.sb("sol", [128, 256], F32) for _ in range(2)],
                        cs=P.sb("cs", [128, 16], F32), junk=P.sb("jk", [128, 128], F32),
                        X=PS[sI * 2], Y=PS[sI * 2 + 1]))

                def stageA_steps(c, S):
                    tok3, kbqg, tT, rg, dm, dmm, cs, X, Y = S["tok3"], S["kbqg"], S["tT"], S["rg"], S["dm"], S["dmm"], S["cs"], S["X"], S["Y"]
                    sl = slice(c * 128, (c + 1) * 128)
                    tr(X[:, 0:128], qc[:, sl], [qc], [X])
                    tr(X[:, 128:256], kc[:, sl], [kc], [X])
                    tr(X[:, 256:384], vc[:, sl], [vc], [X])
                    memset("dve", cs[:, 0:2], 0.0, [cs])
                    act(S["junk"][:], X[:, 0:128], AF.Square, [X, cs], [S["junk"], cs], accum=cs[:, 0:1])
                    act(S["junk"][:], X[:, 128:256], AF.Square, [X, cs], [S["junk"], cs], accum=cs[:, 1:2])
                    rstd_cols(cs[:, 0:2], cs[:, 4:6], cs[:, 2:4], 2, 1.0, [cs])
                    ts("dve", tok3[:, 0:128], X[:, 0:128], cs[:, 4:5], 128.0 ** -0.5, ALU.mult, ALU.mult, [X, cs], [tok3])
                    act(tok3[:, 128:256], X[:, 128:256], AF.Copy, [X, cs], [tok3], scale=cs[:, 5:6])
                    cp("act", tok3[:, 256:384], X[:, 256:384], [X], [tok3])
                    yield
                    gcol = gall[:, c, h:h + 1]
                    bcol = beta[:, c, h:h + 1]
                    ts("dve", rg[:, 0:129], Mgt1, gcol, None, ALU.mult, None, [gall, cf], [rg])
                    ts("dve", rg[:, 129:258], Mle1, gcol, None, ALU.mult, None, [gall, cf], [rg])
                    mm(Y[:, 0:129], Mle, rg[:, 0:129], True, True, [rg, cf], [Y])
                    mm(Y[:, 256:385], Mgt, rg[:, 129:258], True, True, [rg, cf], [Y])
                    cp("dve", cs[:, 6:7], Y[:, 128:129], [Y], [cs])
                    act(cs[:, 7:8], Y[:, 128:129], AF.Exp, [Y], [cs])
                    act(cs[:, 8:9], Y[:, 384:385], AF.Exp, [Y], [cs])
                    tt("dve", cs[:, 9:10], cs[:, 6:7], Y[:, 384:385], ALU.add, [Y, cs], [cs])
                    act(egl_st[:, c:c + 1], cs[:, 9:10], AF.Exp, [cs], [egl_st.part(c)])
                    act(dm[:, 0:128], Y[:, 0:128], AF.Exp, [Y], [dm])
                    act(dm[:, 128:256], Y[:, 256:384], AF.Exp, [Y], [dm])
                    tt("pool", dmm[:, 0:128], dm[:, 0:128], nMgt, ALU.mult, [dm, cf], [dmm])
                    tt("pool", dmm[:, 128:256], dm[:, 128:256], nMlt, ALU.mult, [dm, cf], [dmm])
                    tt("pool", dmm[:, 256:384], dm[:, 128:256], Mle, ALU.mult, [dm, cf], [dmm])
                    yield
                    ts("dve", kbqg[:, 0:128], tok3[:, 128:256], bcol, None, ALU.mult, None, [tok3, beta], [kbqg])
                    ts("dve", kbqg[:, 128:256], tok3[:, 0:128], cs[:, 7:8], None, ALU.mult, None, [tok3, cs], [kbqg])
                    act(kend_st[:, c, :], tok3[:, 128:256], AF.Copy, [tok3, cs], [kend_st.part(c)], scale=cs[:, 8:9])
                    sol0 = S["sol"][0]
                    ts("dve", sol0[:, 0:128], tok3[:, 256:384], bcol, None, ALU.mult, None, [tok3, beta], [sol0])
                    ts("dve", sol0[:, 128:256], kbqg[:, 0:128], cs[:, 7:8], None, ALU.mult, None, [kbqg, cs], [sol0])
                    tr(X[:, 0:128], tok3[:, 128:256], [tok3], [X])
                    tr(X[:, 128:256], kbqg[:, 0:128], [kbqg], [X])
                    tr(X[:, 256:384], tok3[:, 0:128], [tok3], [X])
                    tr(X[:, 384:512], kbqg[:, 128:256], [kbqg], [X])
                    cp("act", tT[:, :], X[:, 0:384], [X], [tT])
                    cp("dve", qgT_st[:, c, :], X[:, 384:512], [X], [qgT_st.part(c)])
                    yield
                    mm(X[:, 0:128], tT[:, 128:256], tT[:, 0:128], True, True, [tT], [X])
                    mm(X[:, 128:256], tT[:, 0:128], tT[:, 128:256], True, True, [tT], [X])
                    mm(X[:, 256:384], tT[:, 0:128], tT[:, 256:384], True, True, [tT], [X])
                    pp0 = S["pp"][0]
                    tt("dve", pp0[:, 0:256], X[:, 0:256], dmm[:, 0:256], ALU.mult, [X, dmm], [pp0])
                    tt("dve", qkT_st[:, c, :], X[:, 256:384], dmm[:, 256:384], ALU.mult, [X, dmm], [qkT_st.part(c)])
                    yield
                    for k in range(7):
                        ppk = S["pp"][k % 2]
                        ppn = S["pp"][(k + 1) % 2]
                        solk = S["sol"][k % 2]
                        soln = S["sol"][(k + 1) % 2]
                        mm(Y[:, 0:256], ppk[:, 128:256], solk[:, :], True, True, [ppk, solk], [Y])
                        if k < 6:
                            mm(Y[:, 256:384], ppk[:, 128:256], ppk[:, 0:128], True, True, [ppk], [Y])
                            mm(Y[:, 384:512], ppk[:, 0:128], ppk[:, 128:256], True, True, [ppk], [Y])
                        tt("dve", soln[:, :], solk[:, :], Y[:, 0:256], ALU.add, [solk, Y], [soln])
                        if k < 6:
                            cp("act", ppn[:, :], Y[:, 256:512], [Y], [ppn])
                        yield
                    solf = S["sol"][1]
                    cp("pool", u_st[:, c, :], solf[:, 0:128], [solf], [u_st.part(c)])
                    tr(X[:, 0:128], solf[:, 128:256], [solf], [X])
                    cp("act", wT_st[:, c, :], X[:, 0:128], [X], [wT_st.part(c)])
                    yield

                for c0 in range(0, NT, NS):
                    gens = [stageA_steps(c0 + i, slots[i]) for i in range(NS)]
                    alive = True
                    while alive:
                        alive = False
                        for g in gens:
                            try:
                                next(g)
                                alive = True
                            except StopIteration:
                                pass
                for S in slots:
                    P.release(S["tok3"], S["kbqg"], S["tT"], S["rg"], S["dm"], S["dmm"], *S["pp"], *S["sol"], S["cs"], S["junk"])
                P.release(*qkv)
                Sf = P.sb("Sf", [128, 128], F32)
                Sb = P.sb("Sb", [128, 128], BF16)
                vnb = [P.sb("vnb", [128, 128], BF16) for _ in range(2)]
                fst = [P.sb("fst", [128, 4], F32) for _ in range(2)]
                ftmp = [P.sb("ftmp", [128, 384], F32) for _ in range(2)]
                memset("dve", Sf[:], 0.0, [Sf])
                memset("dve", Sb[:], 0.0, [Sb])
                A1, A2, A3, A4 = PS[4], PS[5], PS[6], PS[7]
                for c in range(NT):
                    vn = vnb[c % 2]
                    mm(A1[:, 0:128], wT_st[:, c, :], Sb[:], True, True, [wT_st.part(c), Sb], [A1])
                    tt("dve", vn[:], u_st[:, c, :], A1[:, 0:128], ALU.subtract, [u_st.part(c), A1], [vn])
                    mm(A2[:, 0:128], qgT_st[:, c, :], Sb[:], True, False, [qgT_st.part(c), Sb], [A2])
                    mm(A2[:, 0:128], qkT_st[:, c, :], vn[:], False, True, [qkT_st.part(c), vn], [A2])
                    mm(A3[:, 0:128], kend_st[:, c, :], vn[:], True, True, [kend_st.part(c), vn], [A3])
                    stt("dve", Sf[:], Sf[:], egl_st[:, c:c + 1], A3[:, 0:128], ALU.mult, ALU.add, [Sf, egl_st.part(c), A3], [Sf])
                    cp("act", Sb[:], Sf[:], [Sf], [Sb])
                    st, tmp = fst[c % 2], ftmp[c % 2]
                    finish_head(A2[:, 0:128], zh[:, c, :], [zh], prl[:, PR_GDNN - R0:PR_GDNN - R0 + 128], oTh, c, st, tmp, A2)
                    tr(A4[:, 0:128], tmp[:, 256:384], [tmp], [A4])
                    cp("act", oTh[:, c * 128:(c + 1) * 128], A4[:, 0:128], [A4], [oTh])
                P.dma("act", oT_d[h], oTh[:], reads=[oTh], writes=[oT_d.part(h)])
                P.release(Sf, Sb, *vnb, *fst, *ftmp, u_st, wT_st, qgT_st, qkT_st, kend_st, egl_st, oTh, zh)
            P.release(ba, gall, beta, ea)

        if "gla" in parts:
            set_wb(320)
            lrT = P.sb("lrT", [17, T], F32)
            wga = P.sb("wga", [17, 320], F32)
            P.dma("sp", wga[:], wg_d[l], writes=[wga])
            memset("dve", lrT[:], 1.0, [lrT])
            fm_proj(GLA_LR0, 16, lambda tb, ps: cp("act", lrT[0:16, tb * 512:(tb + 1) * 512], ps[0:16, :], [ps], [lrT]))
            spg = P.sb("spg", [128, NT, 320], F32)
            for tI in range(NT):
                ps = prj.next()
                mm(ps[:, 0:320], lrT[0:17, tI * 128:(tI + 1) * 128], wga[0:17, :], True, True, [lrT, wga], [ps])
                act(spg[:, tI, :], ps[:, 0:320], AF.Softplus, [ps], [spg], scale=-1.0)
            P.release(lrT, wga)
            for h in range(5):
                qT = P.sb("glaqT", [64, T], F32)
                kT = P.sb("glakT", [64, T], F32)
                fm_proj(GLA_Q0 + h * 64, 64, lambda tb, ps: cp("act", qT[:, tb * 512:(tb + 1) * 512], ps[0:64, :], [ps], [qT]))
                fm_proj(GLA_K0 + h * 64, 64, lambda tb, ps: cp("act", kT[:, tb * 512:(tb + 1) * 512], ps[0:64, :], [ps], [kT]))
                kvg = P.sb("kvg", [128, NT, 320], BF16)
                tm_proj([(GLA_K0 + h * 64, 64), (GLA_V0 + h * 128, 128), (GLA_G0 + h * 128, 128)],
                        lambda tI, ps: cp("act", kvg[:, tI, :], ps[:, 0:320], [ps], [kvg]))
                oTh = P.sb("oTh", [128, T], BF16)
                Sf = P.sb("gSf", [64, 128], F32)
                Sb = P.sb("gSb", [64, 128], BF16)
                memset("dve", Sf[:], 0.0, [Sf])
                memset("dve", Sb[:], 0.0, [Sb])
                NB = 2
                eT = [P.sb("eT", [64, 256], F32) for _ in range(NB)]
                qk = [P.sb("qkd", [64, 256], BF16) for _ in range(NB)]
                kiv = [P.sb("kiv", [128, 64], BF16) for _ in range(NB)]
                etok = [P.sb("etok", [128, 64], F32) for _ in range(NB)]
                att = [P.sb("att", [128, 128], BF16) for _ in range(NB)]
                fst = [P.sb("fst", [128, 4], F32) for _ in range(NB)]
                ftmp = [P.sb("ftmp", [128, 384], F32) for _ in range(NB)]
                s1 = [P.sb("gs1", [64, 128], F32) for _ in range(NB)]
                B1, B2, B3, B4 = PS[0], PS[1], PS[2], PS[3]
                for c in range(NT):
                    i2 = c % NB
                    sl = slice(c * 128, (c + 1) * 128)
                    sph = spg[:, c, h * 64:(h + 1) * 64]
                    mm(B1[:, 0:64], Mle, sph, True, True, [cf, spg], [B1])
                    mm(B1[0:64, 128:256], sph, Mle, True, True, [cf, spg], [B1])
                    act(eT[i2][:, 0:128], B1[0:64, 128:256], AF.Exp, [B1], [eT[i2]], scale=-1.0 / 16)
                    act(eT[i2][:, 128:256], B1[0:64, 128:256], AF.Exp, [B1], [eT[i2]], scale=1.0 / 16)
                    act(etok[i2][:], B1[:, 0:64], AF.Exp, [B1], [etok[i2]], scale=1.0 / 16)
                    stt("dve", qk[i2][:, 0:128], qT[:, sl], 0.125, eT[i2][:, 0:128], ALU.mult, ALU.mult, [qT, eT[i2]], [qk[i2]])
                    tt("dve", qk[i2][:, 128:256], kT[:, sl], eT[i2][:, 128:256], ALU.mult, [kT, eT[i2]], [qk[i2]])
                    tt("dve", kiv[i2][:], kvg[:, c, 0:64], etok[i2][:], ALU.mult, [kvg, etok[i2]], [kiv[i2]])
                    mm(B2[:, 0:128], qk[i2][:, 128:256], qk[i2][:, 0:128], True, True, [qk[i2]], [B2])
                    tt("dve", att[i2][:], B2[:, 0:128], Mle, ALU.mult, [B2, cf], [att[i2]])
                    mm(B3[:, 0:128], qk[i2][:, 0:128], Sb[:], True, False, [qk[i2], Sb], [B3])
                    mm(B3[:, 0:128], att[i2][:], kvg[:, c, 64:192], False, True, [att[i2], kvg], [B3])
                    mm(B2[0:64, 128:256], kiv[i2][:], kvg[:, c, 64:192], True, True, [kiv[i2], kvg], [B2])
                    ts("dve", s1[i2][:], Sf[:], eT[i2][:, 127:128], None, ALU.mult, None, [Sf, eT[i2]], [s1[i2]])
                    stt("dve", Sf[:], B2[0:64, 128:256], eT[i2][:, 127:128], s1[i2][:], ALU.mult, ALU.add, [B2, eT[i2], s1[i2]], [Sf])
                    cp("act", Sb[:], Sf[:], [Sf], [Sb])
                    st, tmp = fst[i2], ftmp[i2]
                    finish_head(B3[:, 0:128], kvg[:, c, 192:320], [kvg], prl[:, PR_GLAN - R0:PR_GLAN - R0 + 128], oTh, c, st, tmp, B3)
                    tr(B4[:, 0:128], tmp[:, 256:384], [tmp], [B4])
                    cp("act", oTh[:, sl], B4[:, 0:128], [B4], [oTh])
                P.dma("act", oT_d[6 + h], oTh[:], reads=[oTh], writes=[oT_d.part(6 + h)])
                P.release(qT, kT, kvg, oTh, Sf, Sb, *eT, *qk, *kiv, *etok, *att, *fst, *ftmp, *s1)
            P.release(spg)

        if "fox" in parts:
            set_wb(384)
            vsb = P.sb("vsb", [128, NT, 5, 129], BF16)
            fsb = P.sb("fsb", [128, NT, 5], F32)
            memset("dve", vsb[:], 1.0, [vsb])

            def cons_v1(tI, ps):
                cp("act", vsb[:, tI, 0:4, 0:128], ps[:, 0:512].rearrange("p (h d) -> p h d", h=4), [ps], [vsb])

            def cons_v2(tI, ps):
                cp("act", vsb[:, tI, 4, 0:128], ps[:, 0:128], [ps], [vsb])
                cp("dve", fsb[:, tI, :], ps[:, 128:133], [ps], [fsb])
            tm_proj([(FOX_V0, 384)], lambda tI, ps: cp("act", vsb[:, tI, 0:3, 0:128], ps[:, 0:384].rearrange("p (h d) -> p h d", h=3), [ps], [vsb]))
            tm_proj([(FOX_V0 + 384, 256), (FOX_F0, 5)], lambda tI, ps: (
                cp("act", vsb[:, tI, 3:5, 0:128], ps[:, 0:256].rearrange("p (h d) -> p h d", h=2), [ps], [vsb]),
                cp("dve", fsb[:, tI, :], ps[:, 256:261], [ps], [fsb])))
            for tI in range(NT):
                tt("dve", fsb[:, tI, :], fsb[:, tI, :], prl[:, PR_FB - R0:PR_FB - R0 + 5], ALU.add, [fsb, prl], [fsb])
            act(fsb[:], fsb[:], AF.Softplus, [fsb], [fsb], scale=-1.0)
            cpos = P.sb("cpos", [128, NT, 5], F32)
            offc = P.sb("offc", [128, NT, 5], F32)
            totc = P.sb("totc", [128, NT, 5], F32)
            C1 = PS[0]
            fs2 = fsb[:].rearrange("p t h -> p (t h)")
            mm(C1[:, 0:80], Mle, fs2, True, True, [cf, fsb], [C1])
            mm(C1[:, 128:208], ones, fs2, True, True, [cf, fsb], [C1])
            cp("dve", totc[:].rearrange("p t h -> p (t h)"), C1[:, 128:208], [C1], [totc])
            memset("dve", offc[:, 0, :], 0.0, [offc])
            for tI in range(1, NT):
                tt("dve", offc[:, tI, :], offc[:, tI - 1, :], totc[:, tI - 1, :], ALU.add, [offc, totc], [offc])
            tt("dve", cpos[:].rearrange("p t h -> p (t h)"), C1[:, 0:80], offc[:].rearrange("p t h -> p (t h)"), ALU.add, [C1, offc], [cpos])
            crow = P.sb("crow", [5, T], F32)
            totr = P.sb("totr", [5, NT], F32)
            offr = P.sb("offr", [5, NT], F32)
            for tI in range(NT):
                ps = PS[1 + tI % 2]
                mm(ps[0:5, 0:129], fsb[:, tI, :], Mle1, True, True, [cf, fsb], [ps])
                cp("act", crow[:, tI * 128:(tI + 1) * 128], ps[0:5, 0:128], [ps], [crow])
                cp("dve", totr[:, tI:tI + 1], ps[0:5, 128:129], [ps], [totr])
            memset("dve", offr[:, 0:1], 0.0, [offr])
            for tI in range(1, NT):
                tt("dve", offr[:, tI:tI + 1], offr[:, tI - 1:tI], totr[:, tI - 1:tI], ALU.add, [offr, totr], [offr])
            relc = P.sb("relc", [5, NT], F32)
            for qb in range(4):
                ts("dve", relc[:, qb * 4:(qb + 1) * 4], offr[:, qb * 4:(qb + 1) * 4], offr[:, qb * 4:qb * 4 + 1], None, ALU.subtract, None, [offr], [relc])
            for tI in range(NT):
                ts("dve", crow[:, tI * 128:(tI + 1) * 128], crow[:, tI * 128:(tI + 1) * 128], relc[:, tI:tI + 1], -1.0, ALU.add, ALU.mult, [crow, relc], [crow])
            R3 = P.sb("R3", [69, T], BF16)
            rres = P.sb("rres", [5, T], F32)
            rtmp = P.sb("rtmp", [5, T], F32)
            rb = [P.sb("rb", [5, T], BF16) for _ in range(2)]
            memset("dve", R3[:], 0.0, [R3])
            cp("dve", R3[0:5, :], crow[:, :], [crow], [R3])
            cp("dve", rtmp[:], R3[0:5, :], [R3], [rtmp])
            tt("dve", rres[:], crow[:], rtmp[:], ALU.subtract, [crow, rtmp], [rres])
            cp("dve", rb[0][:], rres[:], [rres], [rb[0]])
            cp("dve", rtmp[:], rb[0][:], [rb[0]], [rtmp])
            tt("dve", rres[:], rres[:], rtmp[:], ALU.subtract, [rres, rtmp], [rres])
            cp("dve", rb[1][:], rres[:], [rres], [rb[1]])
            P.dma("sp", R3[32:37, :], rb[0][:], reads=[rb[0]], writes=[R3])
            P.dma("sp", R3[64:69, :], rb[1][:], reads=[rb[1]], writes=[R3])
            P.release(*rb)
            P.release(crow, totr, offr, relc, rres, rtmp, totc)
            for h in range(5):
                qT = P.sb("fqT", [128, T], BF16)
                kT = P.sb("fkT", [128, T], BF16)
                fm_proj(FOX_Q0 + h * 128, 128, lambda tb, ps: act(qT[:, tb * 512:(tb + 1) * 512], ps[:, :], AF.Copy, [ps], [qT], scale=128.0 ** -0.5))
                fm_proj(FOX_K0 + h * 128, 128, lambda tb, ps: cp("dve", kT[:, tb * 512:(tb + 1) * 512], ps[:, :], [ps], [kT]))
                btab = P.sb("btab", [128, NT, 4], F32)
                for qb in range(4):
                    ts("dve", btab[:, :, qb], cpos[:, :, h], offc[:, 4 * qb, h:h + 1], None, ALU.subtract, None, [cpos, offc], [btab])
                oTh = P.sb("oTh", [128, T], BF16)
                pts = [P.sb("pt", [128, 512], BF16) for _ in range(3)]
                osb = [P.sb("osb", [128, 132], F32) for _ in range(2)]
                sc = Rot(PS[0:2])
                selh = cb[0:69, B_SEL + h * 128:B_SEL + (h + 1) * 128]
                for qb in range(4):
                    accs = [PS[2], PS[3], PS[4], PS[5]]
                    accap = lambda j: accs[j][:, 0:129]
                    for tk in range(4 * qb + 4):
                        j0 = max(0, tk - 4 * qb)
                        q0 = j0 * 128
                        ps = sc.next()
                        diag = tk >= 4 * qb
                        mm(ps[:, q0:512], kT[:, tk * 128:(tk + 1) * 128], qT[:, qb * 512 + q0:(qb + 1) * 512], True, False, [kT, qT], [ps])
                        mm(ps[:, q0:512], selh, R3[0:69, qb * 512 + q0:(qb + 1) * 512], False, not diag, [cb, R3], [ps])
                        if diag:
                            mm(ps[:, q0:q0 + 128], identb, Mneg, False, True, [cb], [ps])
                        pt = pts[tk % 3]
                        act(pt[:, q0:512], ps[:, q0:512], AF.Exp, [ps, btab], [pt], bias=btab[:, tk, qb:qb + 1])
                        for j in range(j0, 4):
                            tq = 4 * qb + j
                            mm(accap(j), pt[:, j * 128:(j + 1) * 128], vsb[:, tk, h, :], tk == 0, tk == tq, [pt, vsb], [accs[j]])
                    for j in range(4):
                        tq = 4 * qb + j
                        ob = osb[j % 2]
                        a = accap(j)
                        P.op("dve", lambda e, ob=ob, a=a: e.reciprocal(ob[:, 128:129], a[:, 128:129]), [accs[j]], [ob])
                        ts("dve", ob[:, 0:128], a[:, 0:128], ob[:, 128:129], None, ALU.mult, None, [accs[j], ob], [ob])
                        pst = PS[6 + j % 2]
                        tr(pst[:, 0:128], ob[:, 0:128], [ob], [pst])
                        cp("act", oTh[:, tq * 128:(tq + 1) * 128], pst[:, 0:128], [pst], [oTh])
                P.dma("act", oT_d[11 + h], oTh[:], reads=[oTh], writes=[oT_d.part(11 + h)])
                P.release(qT, kT, btab, oTh, *pts, *osb)
            P.release(vsb, fsb, cpos, offc, R3)
        set_wb(0)
        P.release(prl, hT)

        oT = P.sb("oTall", [128, 16, T], BF16)
        for ct in range(16):
            P.dma("sp", oT[:, ct, :], oT_d[ct], reads=[oT_d.part(ct)], writes=[oT], cowrite=True)
        wo = P.sb("wo", [128, 16, D], BF16)
        for cbk in range(4):
            load_w(w_out_d, l, 0, 16, cbk * 512, 512, wo, bo=cbk * 512)
        nrow = P.sb("nrow", [128, D], F32)
        P.dma("sp", nrow[:], pr_d[l, :, PR_NPM:PR_NPM + D], writes=[nrow])
        xts = [P.sb("xt", [128, D], F32) for _ in range(2)]
        yts = [P.sb("yt", [128, D], F32) for _ in range(2)]
        sts = [P.sb("st", [128, 8], F32) for _ in range(2)]
        junk = P.sb("junk", [128, 512], F32)
        pr = Rot(PS)
        for tI in range(NT):
            xt, yt, st = xts[tI % 2], yts[tI % 2], sts[tI % 2]
            P.dma("sp", xt[:], src[tI * 128:(tI + 1) * 128, :], reads=[src.part(tI)], writes=[xt])
            import os
            OS_ = int(os.environ.get("OP_STEPS", "9"))
            memset("dve", st[:, 0:4], 0.0, [st])
            for cbk in range(4):
                if OS_ < 2:
                    continue
                ps = pr.next()
                for ct in range(16):
                    mm(ps[:, :], oT[:, ct, tI * 128:(tI + 1) * 128], wo[:, ct, cbk * 512:(cbk + 1) * 512], ct == 0, ct == 15, [oT, wo], [ps])
                if OS_ < 3:
                    continue
                act(junk[:], ps[:, :], AF.Square, [ps, st], [junk, st], accum=st[:, cbk:cbk + 1])
                tt("dve", yt[:, cbk * 512:(cbk + 1) * 512], ps[:, :], nrow[:, cbk * 512:(cbk + 1) * 512], ALU.mult, [ps, nrow], [yt])
            if OS_ >= 4:
                P.op("dve", lambda e, st=st: e.tensor_reduce(st[:, 4:5], st[:, 0:4], AX.X, ALU.add), [st], [st])
                rstd_cols(st[:, 4:5], st[:, 6:7], st[:, 5:6], 1, 1.0 / D, [st])
                stt("dve", xt[:], yt[:], st[:, 6:7], xt[:], ALU.mult, ALU.add, [yt, st, xt], [xt])
            P.dma("act", xa_d[tI * 128:(tI + 1) * 128, :], xt[:], reads=[xt], writes=[xa_d.part(tI)])
        P.release(oT, wo, nrow, *xts, *yts, *sts, junk)
        if stop_after == "mix":
            return

        TB = 1024
        NG = 4
        nrow = P.sb("nrow2", [128, D], F32)
        P.dma("sp", nrow[:], pr_d[l, :, PR_NPF:PR_NPF + D], writes=[nrow])
        carry = P.sb("carry", [128, 128, 2], F32)
        for blk in range(T // TB):
            tok0 = blk * TB
            ntt = TB // 128
            h2T = P.sb("h2T", [128, 16, TB], BF16, nparts=ntt)
            norm_phase(xa_d, pcl, PC_NW2, h2T, tok0, ntt)
            ysb = P.sb("ysb", [128, ntt, D], F32, nparts=ntt)
            wup = Rot([P.sb("wup", [128, 16, 256], BF16) for _ in range(3)])
            wdn = Rot([P.sb("wdn", [128, NG, 512], BF16) for _ in range(3)])
            aT = Rot([P.sb("aT", [128, NG, TB], BF16) for _ in range(2)])
            ug = [P.sb("ug", [128, 2, TB + 2], F32) for _ in range(2)]
            cv = [P.sb("cv", [128, 2, TB], F32) for _ in range(2)]
            upr = Rot(PS[0:4])
            dpr = Rot(PS[4:8])
            def up_group(fg):
                    a_t = aT.next()
                    for fi in range(NG):
                        f = fg * NG + fi
                        wt = wup.next()
                        load_w(w_up_d, l, 0, 16, f * 128, 128, wt, bo=0)
                        load_w(w_up_d, l, 0, 16, DFF + f * 128, 128, wt, bo=128)
                        u = ug[f % 2]
                        c = cv[f % 2]
                        for gv in range(2):
                            for tb in range(TB // 512):
                                ps = upr.next()
                                for dt in range(16):
                                    mm(ps[:, :], wt[:, dt, gv * 128:(gv + 1) * 128], h2T[:, dt, tb * 512:(tb + 1) * 512], dt == 0, dt == 15, [wt, h2T], [ps])
                                cp("act", u[:, gv, 2 + tb * 512:2 + (tb + 1) * 512], ps[:, :], [ps], [u])
                            ft = f + 64 * gv
                            if blk == 0:
                                memset("dve", u[:, gv, 0:2], 0.0, [u])
                            else:
                                cp("dve", u[:, gv, 0:2], carry[:, ft, :], [carry], [u])
                        wcf = lambda ft, tap: pcl[:, PC_CONVF + ft * 3 + tap:PC_CONVF + ft * 3 + tap + 1]
                        for gv in range(2):
                            ft = f + 64 * gv
                            eng = "dve"
                            ts(eng, c[:, gv, :], u[:, gv, 2:TB + 2], wcf(ft, 2), pcl[:, PC_BIASF + ft:PC_BIASF + ft + 1], ALU.mult, ALU.add, [u, pcl], [c])
                            stt(eng, c[:, gv, :], u[:, gv, 1:TB + 1], wcf(ft, 1), c[:, gv, :], ALU.mult, ALU.add, [u, pcl, c], [c])
                            stt(eng, c[:, gv, :], u[:, gv, 0:TB], wcf(ft, 0), c[:, gv, :], ALU.mult, ALU.add, [u, pcl, c], [c])
                            if blk < T // TB - 1:
                                cp("dve", carry[:, ft, :], u[:, gv, TB:TB + 2], [u], [carry])
                        act(c[:, 0, :], c[:, 0, :], AF.Gelu_apprx_tanh, [c], [c])
                        tt("dve", a_t[:, fi, :], c[:, 0, :], c[:, 1, :], ALU.mult, [c], [a_t])
                    return a_t

            def down_group(fg, a_t):
                    for cbk in range(4):
                        wd = wdn.next()
                        load_w(w_down_d, l, fg * NG * 128, NG, cbk * 512, 512, wd)
                        for tI in range(ntt):
                            ps = dpr.next()
                            for fi in range(NG):
                                mm(ps[:, :], a_t[:, fi, tI * 128:(tI + 1) * 128], wd[:, fi, :], fi == 0, fi == NG - 1, [a_t, wd], [ps])
                            o = ysb[:, tI, cbk * 512:(cbk + 1) * 512]
                            if fg == 0:
                                cp("act", o, ps[:, :], [ps], [ysb.part(tI)])
                            else:
                                tt("dve", o, o, ps[:, :], ALU.add, [ps, ysb.part(tI)], [ysb.part(tI)])

            nfg = 64 // NG
            a_prev = up_group(0)
            for fg in range(1, nfg):
                a_cur = up_group(fg)
                down_group(fg - 1, a_prev)
                a_prev = a_cur
            down_group(nfg - 1, a_prev)
            P.release(*wup.items, *wdn.items, *aT.items, *ug, *cv, h2T)
            xts = [P.sb("xt", [128, D], F32) for _ in range(2)]
            sts = [P.sb("st", [128, 4], F32) for _ in range(2)]
            junk = P.sb("junk", [128, D], F32)
            for i in range(ntt):
                tI = tok0 // 128 + i
                xt, st = xts[i % 2], sts[i % 2]
                P.dma("sp", xt[:], xa_d[tI * 128:(tI + 1) * 128, :], reads=[xa_d.part(tI)], writes=[xt])
                memset("dve", st[:, 0:1], 0.0, [st])
                act(junk[:], ysb[:, i, :], AF.Square, [ysb.part(i), st], [junk, st], accum=st[:, 0:1])
                rstd_cols(st[:, 0:1], st[:, 2:3], st[:, 1:2], 1, 1.0 / D, [st])
                tt("pool", ysb[:, i, :], ysb[:, i, :], nrow[:], ALU.mult, [ysb.part(i), nrow], [ysb.part(i)])
                stt("dve", xt[:], ysb[:, i, :], st[:, 2:3], xt[:], ALU.mult, ALU.add, [ysb.part(i), st, xt], [xt])
                P.dma("act", dst[tI * 128:(tI + 1) * 128, :], xt[:], reads=[xt], writes=[dst.part(tI)])
            P.release(*xts, *sts, junk, ysb)
            if blk == T // TB - 1:
                P.release(carry)
        P.release(nrow, pcl)

    for l in range(n_layers):
        src = x_d if l == 0 else xb_d
        dst = xb_d if l < n_layers - 1 else y_d
        layer(l, src, dst)
        if stop_after is not None:
            break
    counts = P.emit()
    return nc, counts


def prep_params(inp):
    f32 = np.float32
    pc = np.zeros((2, 128, NPC), f32)
    pr = np.zeros((2, 128, NPR), f32)
    wg = np.zeros((2, 17, 320), f32)
    for l in range(2):
        pc[l, :, PC_NW1:PC_NW1 + 16] = inp["norm_pre_mix"][l].reshape(16, 128).T
        pc[l, :, PC_NW2:PC_NW2 + 16] = inp["norm_pre_ffn"][l].reshape(16, 128).T
        cg = inp["conv_gdn"][l]
        pc[l, :, PC_CONVG:PC_CONVG + 72] = cg.reshape(4, 18, 128).transpose(2, 1, 0).reshape(128, 72)
        cfw = inp["conv_ffn"][l]
        pc[l, :, PC_CONVF:PC_CONVF + 384] = cfw.reshape(3, 128, 128).transpose(2, 1, 0).reshape(128, 384)
        pc[l, :, PC_BIASF:PC_BIASF + 128] = inp["conv_ffn_bias"][l].reshape(128, 128).T
        pr[l, :, PR_NPM:PR_NPM + 2048] = inp["norm_post_mix"][l][None, :]
        pr[l, :, PR_NPF:PR_NPF + 2048] = inp["norm_post_ffn"][l][None, :]
        pr[l, :, PR_GDNN:PR_GDNN + 128] = inp["gdn_norm"][l][None, :]
        pr[l, :, PR_GLAN:PR_GLAN + 128] = inp["gla_norm"][l][None, :]
        pr[l, :, PR_ALOG:PR_ALOG + 6] = inp["gdn_a_log"][l][None, :]
        pr[l, :, PR_DT:PR_DT + 6] = inp["gdn_dt_bias"][l][None, :]
        pr[l, :, PR_FB:PR_FB + 5] = inp["fox_f_bias"][l][None, :]
        wg[l, 0:16] = inp["gla_w_gate"][l]
        wg[l, 16] = inp["gla_b_gate"][l]
    return pc, pr, wg


_CACHE = {}


def kernel(**inputs):
    inp = {k: np.asarray(v) for k, v in inputs.items()}
    if "nc" not in _CACHE:
        _CACHE["nc"] = build(2)[0]
    nc = _CACHE["nc"]
    pc, pr, wg = prep_params(inp)
    cf, cb = make_consts()
    x = np.ascontiguousarray(inp["x"], dtype=np.float32)
    shared = {"w_in": np.ascontiguousarray(inp["w_in"], dtype=np.float32),
              "w_out": np.ascontiguousarray(inp["w_out"], dtype=np.float32),
              "w_up": np.ascontiguousarray(inp["w_up"], dtype=np.float32),
              "w_down": np.ascontiguousarray(inp["w_down"], dtype=np.float32),
              "pcols": pc, "prows": pr, "wg": wg, "cf": cf, "cb": cb}
    in_maps = [dict(shared, x=x[b]) for b in range(8)]
    res = run_bass_kernel_spmd(nc, in_maps, core_ids=list(range(8)))
    return np.stack([np.asarray(r["y"], dtype=np.float32) for r in res.results], axis=0)
```
